# Optimizing a Trainium2 kernel written in Bass

```python
import math
import jax, jax.numpy as jnp
from jax import lax
import numpy as np

D_MODEL = 1024
BATCH = 16
SEQ = 2048
DEPTH = 1

HEAD_DIM = 64
RET_HEADS = 8
ATTN_HEADS = 8
ATTN_KV_HEADS = 2
ATTN_GROUP = ATTN_HEADS // ATTN_KV_HEADS
RET_WIDTH = RET_HEADS * HEAD_DIM
ATTN_WIDTH = ATTN_HEADS * HEAD_DIM
KV_WIDTH = ATTN_KV_HEADS * HEAD_DIM
MIX_WIDTH = RET_WIDTH + ATTN_WIDTH
IN_PROJ_WIDTH = 4 * RET_WIDTH + ATTN_WIDTH + 2 * KV_WIDTH
CHUNK = 128
WINDOW = 128
BLOCK = 128
PLE_DIM = 256
FFN_HIDDEN = ((8 * D_MODEL + 767) // 768) * 256
ALPHA = (2.0 * DEPTH) ** 0.25
BETA = (8.0 * DEPTH) ** -0.25
LN_EPS = 1e-5
GN_EPS = 1e-5
NEG_INF = -1e30

kernel_name = "hybrid_retention_swa_deepnorm_encoder"


def _layer_norm(x, gain, bias):
    xf = x.astype(jnp.float32)
    mu = jnp.mean(xf, axis=-1, keepdims=True)
    var = jnp.mean(jnp.square(xf - mu), axis=-1, keepdims=True)
    return (xf - mu) * lax.rsqrt(var + LN_EPS) * gain.astype(jnp.float32) + bias.astype(jnp.float32)


def _retention_one_direction(q, k, v, log_gamma):
    b, h, s, d = q.shape
    n = s // CHUNK
    q = q.reshape(b, h, n, CHUNK, d)
    k = k.reshape(b, h, n, CHUNK, d)
    v = v.reshape(b, h, n, CHUNK, d)
    idx = jnp.arange(CHUNK, dtype=jnp.float32)
    diff = idx[:, None] - idx[None, :]
    lg = log_gamma[:, None, None]
    decay_in = jnp.where(diff >= 0, jnp.exp(lg * jnp.maximum(diff, 0.0)), 0.0)
    k_decay = jnp.exp(log_gamma[:, None] * (CHUNK - 1.0 - idx)[None, :])
    q_decay = jnp.exp(log_gamma[:, None] * (idx + 1.0)[None, :])
    chunk_decay = jnp.exp(log_gamma * CHUNK)
    scores = jnp.einsum('bhncd,bhnsd->bhncs', q, k) * decay_in[None, :, None]
    y_inner = jnp.einsum('bhncs,bhnse->bhnce', scores, v)
    kv = jnp.einsum('bhnsd,bhnse->bhnde', k * k_decay[None, :, None, :, None], v)

    def step(state, kv_n):
        return state * chunk_decay[None, :, None, None] + kv_n, state

    _, r_prev = lax.scan(step, jnp.zeros((b, h, d, d), q.dtype), jnp.moveaxis(kv, 2, 0))
    r_prev = jnp.moveaxis(r_prev, 0, 2)
    y_cross = jnp.einsum('bhncd,bhnde->bhnce', q * q_decay[None, :, None, :, None], r_prev)
    return (y_inner + y_cross).reshape(b, h, s, d)


def _bidirectional_retention(q, k, v, log_gamma_fwd, log_gamma_bwd, gn_gain, gate):
    b, s, _ = q.shape
    to_heads = lambda t: jnp.transpose(t.astype(jnp.float32).reshape(b, s, RET_HEADS, HEAD_DIM), (0, 2, 1, 3))
    qh, kh, vh = to_heads(q), to_heads(k) * (HEAD_DIM ** -0.5), to_heads(v)
    y_f = _retention_one_direction(qh, kh, vh, log_gamma_fwd)
    flip = lambda t: jnp.flip(t, axis=2)
    y_b = flip(_retention_one_direction(flip(qh), flip(kh), flip(vh), log_gamma_bwd))
    y = y_f + y_b
    mu = jnp.mean(y, axis=-1, keepdims=True)
    var = jnp.mean(jnp.square(y - mu), axis=-1, keepdims=True)
    y = (y - mu) * lax.rsqrt(var + GN_EPS)
    y = jnp.transpose(y, (0, 2, 1, 3)).reshape(b, s, RET_WIDTH) * gn_gain.astype(jnp.float32)
    return y * jax.nn.silu(gate.astype(jnp.float32))


def _alibi_slopes(n_heads):
    return 2.0 ** (-8.0 * (jnp.arange(n_heads, dtype=jnp.float32) + 1.0) / n_heads)


def _windowed_gqa(q, k, v, sink):
    b, s, _ = q.shape
    n = s // BLOCK
    qb = q.astype(jnp.float32).reshape(b, n, BLOCK, ATTN_KV_HEADS, ATTN_GROUP, HEAD_DIM)

    def neighbour_blocks(t):
        tp = jnp.pad(t.astype(jnp.float32).reshape(b, s, ATTN_KV_HEADS, HEAD_DIM),
                     ((0, 0), (BLOCK, BLOCK), (0, 0), (0, 0)))
        tp = tp.reshape(b, n + 2, BLOCK, ATTN_KV_HEADS, HEAD_DIM)
        return jnp.concatenate([tp[:, :-2], tp[:, 1:-1], tp[:, 2:]], axis=2)

    kb, vb = neighbour_blocks(k), neighbour_blocks(v)
    scores = jnp.einsum('bnqhgd,bnshd->bnhgqs', qb, kb) * (HEAD_DIM ** -0.5)
    qi = jnp.arange(BLOCK)
    kj = jnp.arange(3 * BLOCK)
    dist = jnp.abs(kj[None, :] - BLOCK - qi[:, None])
    key_pos = jnp.arange(n)[:, None] * BLOCK - BLOCK + kj[None, :]
    valid = (dist <= WINDOW)[None] & ((key_pos >= 0) & (key_pos < s))[:, None, :]
    slopes = _alibi_slopes(ATTN_HEADS).reshape(ATTN_KV_HEADS, ATTN_GROUP)
    alibi = -slopes[:, :, None, None] * dist.astype(jnp.float32)[None, None]
    scores = jnp.where(valid[None, :, None, None], scores + alibi[None, None], NEG_INF)
    sink_l = sink.astype(jnp.float32).reshape(ATTN_KV_HEADS, ATTN_GROUP)[None, None, :, :, None, None]
    m = jnp.maximum(jnp.max(scores, axis=-1, keepdims=True), sink_l)
    e = jnp.exp(scores - m)
    denom = jnp.sum(e, axis=-1, keepdims=True) + jnp.exp(sink_l - m)
    probs = e / denom
    out = jnp.einsum('bnhgqs,bnshd->bnqhgd', probs, vb)
    return out.reshape(b, s, ATTN_WIDTH)


def setup_inputs(seed: int = 0) -> dict:
    key = jax.random.key(seed)
    ks = jax.random.split(key, 18)
    nrm = lambda k, shape, scale: jax.random.normal(k, shape, jnp.float32) * scale
    base_log2 = -5.0 - jnp.arange(RET_HEADS, dtype=jnp.float32)
    return {
        "x": nrm(ks[0], (BATCH, SEQ, D_MODEL), 1.0),
        "p": nrm(ks[1], (DEPTH, BATCH, SEQ, PLE_DIM), 1.0),
        "w_in": nrm(ks[2], (DEPTH, D_MODEL, IN_PROJ_WIDTH), D_MODEL ** -0.5),
        "ret_decay_fwd": base_log2[None] + nrm(ks[3], (DEPTH, RET_HEADS), 0.1),
        "ret_decay_bwd": base_log2[None] + nrm(ks[4], (DEPTH, RET_HEADS), 0.1),
        "ret_gn_gain": 1.0 + nrm(ks[5], (DEPTH, RET_WIDTH), 0.02),
        "attn_sink": nrm(ks[6], (DEPTH, ATTN_HEADS), 0.5),
        "w_out": nrm(ks[7], (DEPTH, MIX_WIDTH, D_MODEL), BETA * MIX_WIDTH ** -0.5),
        "ln1_gain": 1.0 + nrm(ks[8], (DEPTH, D_MODEL), 0.02),
        "ln1_bias": nrm(ks[9], (DEPTH, D_MODEL), 0.02),
        "w_ffn_gate": nrm(ks[10], (DEPTH, D_MODEL, FFN_HIDDEN), D_MODEL ** -0.5),
        "w_ffn_up": nrm(ks[11], (DEPTH, D_MODEL, FFN_HIDDEN), D_MODEL ** -0.5),
        "w_ffn_down": nrm(ks[12], (DEPTH, FFN_HIDDEN, D_MODEL), BETA * FFN_HIDDEN ** -0.5),
        "w_ple_proj": nrm(ks[13], (DEPTH, PLE_DIM, D_MODEL), BETA * PLE_DIM ** -0.5),
        "w_ple_gate": nrm(ks[14], (DEPTH, D_MODEL, D_MODEL), D_MODEL ** -0.5),
        "ln2_gain": 1.0 + nrm(ks[15], (DEPTH, D_MODEL), 0.02),
        "ln2_bias": nrm(ks[16], (DEPTH, D_MODEL), 0.02),
    }


def reference(x, p, w_in, ret_decay_fwd, ret_decay_bwd, ret_gn_gain, attn_sink, w_out,
              ln1_gain, ln1_bias, w_ffn_gate, w_ffn_up, w_ffn_down, w_ple_proj, w_ple_gate,
              ln2_gain, ln2_bias):
    split_points = [RET_WIDTH, 2 * RET_WIDTH, 3 * RET_WIDTH, 4 * RET_WIDTH,
                    4 * RET_WIDTH + ATTN_WIDTH, 4 * RET_WIDTH + ATTN_WIDTH + KV_WIDTH]
    h = x.astype(jnp.float32)
    for i in range(DEPTH):
        u = jnp.einsum('bsd,de->bse', h, w_in[i].astype(jnp.float32))
        rq, rk, rv, rg, aq, ak, av = jnp.split(u, split_points, axis=-1)
        lg_f = jnp.log1p(-jnp.exp2(ret_decay_fwd[i].astype(jnp.float32)))
        lg_b = jnp.log1p(-jnp.exp2(ret_decay_bwd[i].astype(jnp.float32)))
        y_ret = _bidirectional_retention(rq, rk, rv, lg_f, lg_b, ret_gn_gain[i], rg)
        y_att = _windowed_gqa(aq, ak, av, attn_sink[i])
        mix = jnp.einsum('bse,ed->bsd', jnp.concatenate([y_ret, y_att], axis=-1),
                         w_out[i].astype(jnp.float32))
        h = _layer_norm(ALPHA * h + mix, ln1_gain[i], ln1_bias[i])
        g = jnp.einsum('bsd,df->bsf', h, w_ffn_gate[i].astype(jnp.float32))
        up = jnp.einsum('bsd,df->bsf', h, w_ffn_up[i].astype(jnp.float32))
        ffn = jnp.einsum('bsf,fd->bsd', jax.nn.silu(g) * up, w_ffn_down[i].astype(jnp.float32))
        ple = jnp.einsum('bsr,rd->bsd', p[i].astype(jnp.float32), w_ple_proj[i].astype(jnp.float32))
        ple_gate = jax.nn.sigmoid(jnp.einsum('bsd,de->bse', h, w_ple_gate[i].astype(jnp.float32)))
        h = _layer_norm(ALPHA * h + ffn + ple_gate * ple, ln2_gain[i], ln2_bias[i])
    return h.astype(x.dtype)
```

```python
import contextlib
import math
import numpy as np
import concourse.bass as bass
import concourse.mybir as mybir
from concourse.bass_utils import run_bass_kernel_spmd

F32 = mybir.dt.float32
BF16 = mybir.dt.bfloat16
AF = mybir.ActivationFunctionType
ALU = mybir.AluOpType
AX = mybir.AxisListType

D = 1024
FF = 2816
NJB = FF // 128
INW = 2816
PLE = 256
ALPHA = 2.0 ** 0.25
LN_EPS = 1e-5
GN_EPS = 1e-5
LN2C = math.log(2.0)
LN8 = math.log(0.125)
BIG = 1.0e9

ENGS = ("pe", "act", "dve", "pool", "sp")


class _Op:
    __slots__ = ("eng", "fn", "deps", "dma_sem", "token", "signal", "idx", "is_dma")


class Prog:
    def __init__(self, nc):
        self.nc = nc
        self.ops = []
        self.last_w = {}
        self.readers = {}
        self.phys = {}
        self.phys_count = {}
        self.free_phys_q = {}
        self.phys_q = {}
        self.pending_barrier = {}

    def op(self, eng, fn, reads=(), writes=(), dma=None):
        o = _Op()
        o.eng = eng
        o.fn = fn
        o.is_dma = dma is not None
        o.dma_sem = dma
        o.signal = o.is_dma
        o.idx = len(self.ops)
        deps = []
        for k in reads:
            w = self.last_w.get(k)
            if w is not None:
                deps.append((w, "raw"))
        for k in writes:
            w = self.last_w.get(k)
            if w is not None:
                deps.append((w, "waw"))
            for r in self.readers.get(k, ()):
                deps.append((r, "war"))
        bar = self.pending_barrier.pop(eng, None)
        if bar is not None:
            for d in bar:
                deps.append((d, "bar"))
        od = []
        seen = set()
        for (d, kind) in deps:
            if d is o or d.idx in seen:
                continue
            if d.eng == eng and not d.is_dma and not o.is_dma:
                if kind != "raw" or eng == "pe":
                    continue
            seen.add(d.idx)
            od.append(d)
            d.signal = True
        o.deps = od
        for k in reads:
            self.readers.setdefault(k, []).append(o)
        for k in writes:
            self.last_w[k] = o
            self.readers[k] = []
        if o.is_dma:
            pid = self.phys.get(dma)
            if pid is None:
                fp = self.free_phys_q.setdefault(eng, [])
                pid = fp.pop() if fp else len(self.phys_count)
                self.phys[dma] = pid
                self.phys_q[pid] = eng
                self.phys_count.setdefault(pid, 0)
            c = self.phys_count[pid] + 16
            self.phys_count[pid] = c
            o.dma_sem = pid
            o.token = (("dma", pid), c)
        self.ops.append(o)
        return o

    def barrier(self):
        tails = {}
        for o in self.ops:
            if o.is_dma:
                tails[("dma", o.dma_sem)] = o
            else:
                tails[("eng", o.eng)] = o
        tl = list(tails.values())
        for e in ENGS:
            self.pending_barrier[e] = list(tl)
        self.last_w = {}
        self.readers = {}
        for pid in sorted(self.phys.values(), reverse=True):
            self.free_phys_q.setdefault(self.phys_q[pid], []).append(pid)
        self.phys = {}

    def emit(self, final_ops=()):
        nc = self.nc
        cnt = {e: 0 for e in ENGS}
        for o in self.ops:
            if not o.is_dma and o.signal:
                cnt[o.eng] += 1
                o.token = (("eng", o.eng), cnt[o.eng])
        sem_names = [("eng", e) for e in ENGS] + [("dma", k) for k in self.phys_count]
        with contextlib.ExitStack() as st:
            sems = {}
            for i, sn in enumerate(sem_names):
                sems[sn] = st.enter_context(nc.semaphore("s%d" % i))
            block = st.enter_context(nc.Block())
            per_eng = {e: [o for o in self.ops if o.eng == e] for e in ENGS}

            def body(e):
                def run(engh):
                    waited = {}

                    def wait(tok):
                        sn, v = tok
                        if waited.get(sn, 0) >= v:
                            return
                        waited[sn] = v
                        engh.wait_ge(sems[sn], v)

                    for o in per_eng[e]:
                        need = {}
                        for d in o.deps:
                            sn, v = d.token
                            if need.get(sn, 0) < v:
                                need[sn] = v
                        for sn, v in need.items():
                            wait((sn, v))
                        inst = o.fn(engh)
                        if o.is_dma:
                            inst.then_inc(sems[("dma", o.dma_sem)], 16)
                        elif o.signal:
                            inst.then_inc(sems[o.token[0]], 1)
                    if e == "sp":
                        for fo in final_ops:
                            wait(fo.token)
                return run

            block.tensor(body("pe"))
            block.scalar(body("act"))
            block.vector(body("dve"))
            block.gpsimd(body("pool"))
            block.sync(body("sp"))


def _const_tables():
    i = np.arange(128, dtype=np.float32)
    s = i[:, None]
    c = i[None, :]
    tf = np.where(c >= s, c - s, BIG).astype(np.float32)
    tb = np.where(s >= c, s - c, BIG).astype(np.float32)
    cp1 = np.broadcast_to(c + 1.0, (128, 128)).astype(np.float32)
    c128m = np.broadcast_to(128.0 - c, (128, 128)).astype(np.float32)
    bdt = np.where((np.arange(128)[:, None] // 64) == (np.arange(128)[None, :] // 64), 128.0, BIG).astype(np.float32)
    ident = np.eye(128, dtype=np.float32)
    tabs = np.stack([tf, tb, cp1, c128m, bdt, ident], axis=1)
    colv = np.stack([127.0 - i, i], axis=1).astype(np.float32)
    slopes = 2.0 ** (-(np.arange(8, dtype=np.float64) + 1.0))
    j = np.arange(128)[:, None]
    q = np.arange(128)[None, :]
    attb = np.zeros((3, 128, 8, 128), dtype=np.float32)
    for h in range(8):
        dl = 128 + q - j
        attb[0, :, h, :] = np.where(j >= q, -slopes[h] * dl, -1.0e30)
        attb[1, :, h, :] = -slopes[h] * np.abs(q - j)
        dr = 128 + j - q
        attb[2, :, h, :] = np.where(j <= q, -slopes[h] * dr, -1.0e30)
    import ml_dtypes
    a = attb.transpose(1, 0, 2, 3).reshape(128, 3, 2, 2, 2, 128)
    a = a.transpose(0, 1, 2, 4, 3, 5).reshape(128, 3, 2, 512)
    hi = a.astype(ml_dtypes.bfloat16).astype(np.float32)
    lo = (a - hi).astype(ml_dtypes.bfloat16).astype(np.float32)
    attb2 = np.ascontiguousarray(np.stack([hi, lo], axis=1))
    return np.ascontiguousarray(tabs), colv, attb2


class Ops:
    def __init__(self, P):
        self.P = P

    def mms(self, items, reads, writes):
        items = [tuple(it) for it in items]

        def fn(e):
            inst = None
            for it in items:
                o, l, r, s0, s1 = it[:5]
                if len(it) > 5 and it[5]:
                    inst = e.matmul(o, lhsT=l, rhs=r, start=s0, stop=s1, skip_group_check=True)
                else:
                    inst = e.matmul(o, lhsT=l, rhs=r, start=s0, stop=s1)
            return inst
        return self.P.op("pe", fn, reads, writes)

    def acc(self, out, pairs, reads, writes):
        pairs = list(pairs)
        n = len(pairs)
        return self.mms([(out, l, r, i == 0, i == n - 1) for i, (l, r) in enumerate(pairs)], reads, writes)

    def trs(self, items, ident, reads, writes):
        items = list(items)

        def fn(e):
            inst = None
            for (o, i_) in items:
                inst = e.transpose(o, i_, ident)
            return inst
        return self.P.op("pe", fn, reads, writes)

    def act(self, out, in_, func, reads, writes, scale=None, bias=None):
        kw = {}
        if scale is not None:
            kw["scale"] = scale
        if bias is not None:
            kw["bias"] = bias
        return self.P.op("act", lambda e: e.activation(out=out, in_=in_, func=func, **kw), reads, writes)

    def acts(self, items, reads, writes):
        items = list(items)

        def fn(e):
            inst = None
            for (o, i_, f_) in items:
                inst = e.activation(out=o, in_=i_, func=f_)
            return inst
        return self.P.op("act", fn, reads, writes)

    def tt(self, out, in0, in1, alu, reads, writes, eng="dve"):
        return self.P.op(eng, lambda e: e.tensor_tensor(out=out, in0=in0, in1=in1, op=alu), reads, writes)

    def stt(self, out, in0, scalar, in1, op0, op1, reads, writes):
        return self.P.op("dve", lambda e: e.scalar_tensor_tensor(out=out, in0=in0, scalar=scalar, in1=in1, op0=op0, op1=op1),
                         reads, writes)

    def ts(self, out, in0, s1, s2, op0, op1, reads, writes):
        if s2 is None:
            return self.P.op("dve", lambda e: e.tensor_scalar(out=out, in0=in0, scalar1=s1, scalar2=None, op0=op0), reads, writes)
        return self.P.op("dve", lambda e: e.tensor_scalar(out=out, in0=in0, scalar1=s1, scalar2=s2, op0=op0, op1=op1), reads, writes)

    def copy(self, eng, out, in_, reads, writes, scale=None):
        if eng == "act":
            if scale is not None:
                return self.P.op("act", lambda e: e.activation(out=out, in_=in_, func=AF.Copy, scale=scale), reads, writes)
            return self.P.op("act", lambda e: e.activation(out=out, in_=in_, func=AF.Copy), reads, writes)
        assert scale is None
        return self.P.op(eng, lambda e: e.tensor_copy(out=out, in_=in_), reads, writes)

    def memsets(self, items, writes, eng="dve"):
        items = list(items)

        def fn(e):
            inst = None
            for (ap, v) in items:
                inst = e.memset(ap, v)
            return inst
        return self.P.op(eng, fn, (), writes)

    def reduce(self, out, in_, alu, reads, writes):
        return self.P.op("dve", lambda e: e.tensor_reduce(out=out, in_=in_, axis=AX.X, op=alu), reads, writes)

    def recip(self, out, in_, reads, writes):
        return self.P.op("dve", lambda e: e.reciprocal(out=out, in_=in_), reads, writes)

    def bnstats(self, items, reads, writes):
        items = list(items)

        def fn(e):
            inst = None
            for (o, i_) in items:
                inst = e.bn_stats(out=o, in_=i_)
            return inst
        return self.P.op("dve", fn, reads, writes)

    def bnaggr(self, out, in_, reads, writes):
        return self.P.op("dve", lambda e: e.bn_aggr(out=out, in_=in_), reads, writes)

    def dma(self, eng, out, in_, reads, writes, sem, slow=False):
        if slow:
            return self.P.op(eng, lambda e: e.dma_start(out=out, in_=in_, allow_slow_non_contiguous=True), reads, writes, dma=sem)
        return self.P.op(eng, lambda e: e.dma_start(out=out, in_=in_), reads, writes, dma=sem)

    def acts_sb(self, items, reads, writes):
        items = list(items)

        def fn(e):
            inst = None
            for (o, i_, f_, sc_, bi_) in items:
                inst = e.activation(out=o, in_=i_, func=f_, scale=sc_, bias=bi_)
            return inst
        return self.P.op("act", fn, reads, writes)


def build(NSEQ, S, debug=False):
    NT = S // 128
    NG = S // 512
    TOK = NSEQ * S
    NGT = NSEQ * NG
    NTT = NSEQ * NT
    nc = bass.Bass("TRN2", target_bir_lowering=False)

    def din(name, shape, dt=F32):
        return nc.dram_tensor(name, list(shape), dt, kind="ExternalInput").ap()

    x = din("x", [TOK, D])
    pin = din("p", [TOK, PLE])
    w_in = din("w_in", [D, INW])
    dfw = din("ret_decay_fwd", [1, 8])
    dbw = din("ret_decay_bwd", [1, 8])
    gng = din("ret_gn_gain", [1, 512])
    sink = din("attn_sink", [1, 8])
    w_out = din("w_out", [D, D])
    ln1g = din("ln1_gain", [1, D])
    ln1b = din("ln1_bias", [1, D])
    w_g = din("w_ffn_gate", [D, FF])
    w_u = din("w_ffn_up", [D, FF])
    w_d = din("w_ffn_down", [FF, D])
    w_pe = din("w_ple_proj", [PLE, D])
    w_pg = din("w_ple_gate", [D, D])
    ln2g = din("ln2_gain", [1, D])
    ln2b = din("ln2_bias", [1, D])
    tabs_d = din("c_tabs", [128, 6, 128])
    colv_d = din("c_colv", [128, 2])
    attb_d = din("c_attb", [128, 2, 3, 2, 512])
    out = nc.dram_tensor("out", [TOK, D], F32, kind="ExternalOutput").ap()

    skind = "ExternalOutput" if debug else "Internal"

    def dscr(name, shape, dt):
        return nc.dram_tensor(name, list(shape), dt, kind=skind).ap()

    QT = dscr("s_qt", [NGT, 128, 4 * 512], BF16)
    KT = dscr("s_kt", [NGT, 128, 4 * 512], BF16)
    AQT = dscr("s_aqt", [NGT, 128, 4 * 512], BF16)
    VV = dscr("s_v", [NGT, 128, 4 * 512], BF16)
    GGS = dscr("s_gg", [NGT, 128, 4 * 512], BF16)
    AKT = dscr("s_akt", [NSEQ, 128, S], BF16)
    AV1 = dscr("s_av1", [NSEQ, NT, 128, 130], BF16)
    RF = dscr("s_rf", [NSEQ, 128, NT * 512], BF16)
    RB = dscr("s_rb", [NSEQ, 128, NT * 512], BF16)
    H32 = dscr("s_h32", [NTT, 128, D], F32)
    HT = dscr("s_ht", [NGT, 128, 8 * 512], BF16)
    ACTS = dscr("s_act", [NGT, 128, NJB * 512], BF16)

    P = Prog(nc)
    O = Ops(P)
    final_ops = []

    def fl(ap3):
        return ap3.rearrange("p a b -> p (a b)")

    def h8(ap2):
        return ap2.rearrange("p (h d) -> p h d", h=8)

    def q4v(ap2):
        return ap2.rearrange("p (a b) -> p a b", a=4)

    with contextlib.ExitStack() as top:
        def sbuf(stack, name, shape, dt):
            return stack.enter_context(nc.sbuf_tensor(name, list(shape), dt))

        ps = top.enter_context(nc.psum_tensor("ps", [128, 8 * 512], F32))

        def bank(i):
            return ps[:, i * 512:(i + 1) * 512]

        def PK(i):
            return ("ps", i)

        tabs = sbuf(top, "tabs", [128, 6, 128], F32)
        colv = sbuf(top, "colv", [128, 2], F32)
        dd = sbuf(top, "dd", [128, 24], F32)
        de = sbuf(top, "de", [128, 24], F32)
        lg = sbuf(top, "lg", [128, 24], F32)
        cst = sbuf(top, "cst", [128, 4], F32)
        ident = tabs[:, 5, :]

        O.dma("sp", tabs[:], tabs_d, [], ["tabs"], "c0")
        O.dma("sp", colv[:], colv_d, [], ["colv"], "c1")
        O.dma("sp", dd[:, 0:8], dfw[0].partition_broadcast(128), [], [("dd", 0)], "c2")
        O.dma("sp", dd[:, 8:16], dbw[0].partition_broadcast(128), [], [("dd", 1)], "c3")
        ddk = [("dd", 0), ("dd", 1), ("dd", 2)]
        for t in range(2):
            O.copy("dve", dd[t * 64:(t + 1) * 64, 16:20], dd[t * 64:(t + 1) * 64, t:8:2], [("dd", 0)], [("dd", 2)])
            O.copy("dve", dd[t * 64:(t + 1) * 64, 20:24], dd[t * 64:(t + 1) * 64, 8 + t:16:2], [("dd", 1)], [("dd", 2)])
        O.memsets([(cst[:, 0:1], LN_EPS), (cst[:, 1:2], GN_EPS), (cst[:, 2:3], LN8), (cst[:, 3:4], 1.0)], ["cst"])
        O.act(de[:], dd[:], AF.Exp, ddk, ["de"], scale=LN2C)
        O.act(lg[:], de[:], AF.Ln, ["de", "cst"], ["lg"], scale=-1.0, bias=cst[:, 3:4])

        def lnorm(z, zkeys, gain, bias, lnkeys, outt, okey, st6, mv, sd, rs, tag):
            k_st, k_mv, k_sd, k_rs = [tag + s_ for s_ in ("st", "mv", "sd", "rs")]
            O.bnstats([(st6[:, 0:6], z[:, 0:512]), (st6[:, 6:12], z[:, 512:1024])], zkeys, [k_st])
            O.bnaggr(mv[:, 0:2], st6[:, 0:12], [k_st], [k_mv])
            O.act(sd[:, 0:1], mv[:, 1:2], AF.Sqrt, [k_mv, "cst"], [k_sd], scale=1.0, bias=cst[:, 0:1])
            O.recip(rs[:, 0:1], sd[:, 0:1], [k_sd], [k_rs])
            O.ts(outt, z, mv[:, 0:1], rs[:, 0:1], ALU.subtract, ALU.mult, zkeys + [k_mv, k_rs], [okey])
            O.tt(outt, outt, gain, ALU.mult, [okey] + lnkeys, [okey])
            O.tt(outt, outt, bias, ALU.add, [okey] + lnkeys, [okey])

        with contextlib.ExitStack() as pa:
            win = sbuf(pa, "win", [128, 8, INW], BF16)
            xs = [sbuf(pa, "xsA%d" % i, [128, D], F32) for i in range(3)]
            xT = [sbuf(pa, "xTA%d" % i, [128, 8, 512], BF16) for i in range(2)]
            qT_st = [sbuf(pa, "qTst%d" % i, [128, 4, 512], BF16) for i in range(2)]
            kT_st = [sbuf(pa, "kTst%d" % i, [128, 4, 512], BF16) for i in range(2)]
            aq_st = [sbuf(pa, "aqst%d" % i, [128, 4, 512], BF16) for i in range(2)]
            ak_st = [sbuf(pa, "akst%d" % i, [128, 512], BF16) for i in range(2)]
            v_st = [sbuf(pa, "vst%d" % i, [128, 4, 512], BF16) for i in range(2)]
            gg_st = [sbuf(pa, "ggst%d" % i, [128, 4, 512], BF16) for i in range(2)]
            av_st = [sbuf(pa, "avst%d" % i, [128, 4, 130], BF16) for i in range(2)]
            kf = [sbuf(pa, "kf%d" % i, [128, 512], BF16) for i in range(2)]
            kb = [sbuf(pa, "kb%d" % i, [128, 512], BF16) for i in range(2)]
            sg = [sbuf(pa, "sgA%d" % i, [128, 512], F32) for i in range(2)]
            kvb = sbuf(pa, "kvb", [128, NT, 512], F32)
            rf_st = sbuf(pa, "rfst", [128, NT, 512], BF16)
            rb_st = sbuf(pa, "rbst", [128, NT, 512], BF16)
            st_f = sbuf(pa, "stf", [128, 512], F32)
            st_b = sbuf(pa, "stb", [128, 512], F32)
            cdf = sbuf(pa, "cdf", [128, 4, 128], F32)
            cdb = sbuf(pa, "cdb", [128, 4, 128], F32)
            dk = sbuf(pa, "dk", [128, 16], F32)
            gnr = sbuf(pa, "gnr", [128, 512], F32)

            w_in_v = w_in.rearrange("(kc p) f -> p kc f", p=128)
            WINK = [("win", kc) for kc in range(8)]
            for kc in range(8):
                O.dma("pool", win[:, kc, :], w_in_v[:, kc, :], [], [("win", kc)], "win")
            O.dma("sp", gnr[:], gng[0].partition_broadcast(128), [], ["gnr"], "c6")

            O.act(dk[:, 0:8], lg[:, 0:8], AF.Exp, ["lg", "colv", "cst"], [("dk", 0)], scale=colv[:, 0:1], bias=cst[:, 2:3])
            O.act(dk[:, 8:16], lg[:, 8:16], AF.Exp, ["lg", "colv", "cst"], [("dk", 1)], scale=colv[:, 1:2], bias=cst[:, 2:3])
            for j in range(4):
                O.act(cdf[:, j, :], tabs[:, 4, :], AF.Exp, ["lg", "tabs"], ["cdf"], scale=lg[:, 16 + j:17 + j])
                O.act(cdb[:, j, :], tabs[:, 4, :], AF.Exp, ["lg", "tabs"], ["cdb"], scale=lg[:, 20 + j:21 + j])
            O.memsets([(fl(rf_st[:]), 0.0), (fl(rb_st[:]), 0.0), (fl(av_st[0][:]), 1.0), (fl(av_st[1][:]), 1.0)],
                      ["rfst", "rbst", ("avst", 0), ("avst", 1)])

            BT = [0, 1]
            ACC = [2, 3, 4, 5]
            KVF, KVB = 6, 7
            cnt = {"acc": 0, "ev": 0}

            def next_acc():
                b = ACC[cnt["acc"] % 4]
                cnt["acc"] += 1
                return b

            def evac(dst, src, pkey, wkey, scale=None):
                cnt["ev"] += 1
                if scale is not None or cnt["ev"] % 2 == 0:
                    O.copy("act", dst, src, [], [pkey, wkey], scale=scale)
                else:
                    O.copy("dve", dst, src, [], [pkey, wkey])

            def scan_step(st, rst_n, cd, kv_src, kvkeys, skey, rkey, cdkey):
                sv = q4v(st[:])
                dv = q4v(rst_n)
                O.acts([(dv[0:64, :, 0:64], sv[0:64, :, 0:64], AF.Copy), (dv[64:128, :, 64:128], sv[64:128, :, 64:128], AF.Copy)],
                       [skey], [rkey])
                O.tt(st[:], st[:], fl(cd[:]), ALU.mult, [skey, cdkey], [skey])
                O.tt(st[:], st[:], kv_src, ALU.add, [skey], [skey] + kvkeys)

            def chain_X(G):
                gs = G % 2
                for t in range(4):
                    gt = G * 4 + t
                    sl = gt % 3
                    O.dma("sp", xs[sl][:], x[gt * 128:(gt + 1) * 128, :], [], [("xsA", sl)], "xsA%d" % sl)
                    for hb in range(2):
                        O.trs([(bank(BT[hb])[:, q * 128:(q + 1) * 128], xs[sl][:, (hb * 4 + q) * 128:(hb * 4 + q + 1) * 128]) for q in range(4)],
                              ident, [("xsA", sl), "tabs"], [PK(BT[hb])])
                        yield
                        evac(xT[gs][:, hb * 4:(hb + 1) * 4, t * 128:(t + 1) * 128], q4v(bank(BT[hb])), PK(BT[hb]), ("xTA", gs, t, hb))
                        yield

            def group_A(sq, g):
                G = sq * NG + g
                gs = G % 2
                os_ = G % 2
                xTk = [("xTA", gs, t, hb) for t in range(4) for hb in range(2)]

                def fm_block(col, dst, dkey, scale=None):
                    b = next_acc()
                    O.acc(bank(b), [(win[:, kc, col:col + 128], xT[gs][:, kc, :]) for kc in range(8)], WINK + xTk, [PK(b)])
                    evac(dst, bank(b), PK(b), dkey, scale=scale)

                for j in range(4):
                    fm_block(j * 128, qT_st[os_][:, j, :], ("qTst", os_, j))
                    yield
                for j in range(4):
                    fm_block(512 + j * 128, kT_st[os_][:, j, :], ("kTst", os_, j))
                    yield
                for j in range(4):
                    fm_block(2048 + j * 128, aq_st[os_][:, j, :], ("aqst", os_, j), scale=0.125)
                    yield
                fm_block(2560, ak_st[os_][:], ("akst", os_))
                yield

                for t in range(4):
                    n = g * 4 + t
                    ks = (G * 4 + t) % 2
                    xk = [("xTA", gs, t, 0), ("xTA", gs, t, 1)]

                    def tm_mm(col, width):
                        b = next_acc()
                        O.acc(bank(b)[:, 0:width], [(xT[gs][:, kc, t * 128:(t + 1) * 128], win[:, kc, col:col + width]) for kc in range(8)],
                              WINK + xk, [PK(b)])
                        return b

                    b = tm_mm(512, 512)
                    O.tt(h8(kf[ks][:]), h8(bank(b)), dk[:, 0:8].unsqueeze(2).to_broadcast([128, 8, 64]), ALU.mult,
                         [("dk", 0)], [PK(b), ("kf", ks)])
                    O.tt(h8(kb[ks][:]), h8(bank(b)), dk[:, 8:16].unsqueeze(2).to_broadcast([128, 8, 64]), ALU.mult,
                         [("dk", 1)], [PK(b), ("kb", ks)])
                    b = tm_mm(1024, 512)
                    O.copy("act", v_st[os_][:, t, :], bank(b), [], [PK(b), ("vst", os_, t)])
                    b = tm_mm(1536, 512)
                    O.act(sg[ks][:], bank(b), AF.Silu, [], [PK(b), ("sgA", ks)])
                    O.tt(gg_st[os_][:, t, :], sg[ks][:], gnr[:], ALU.mult, [("sgA", ks), "gnr"], [("ggst", os_)])
                    b = tm_mm(2688, 128)
                    O.copy("act", av_st[os_][:, t, :].rearrange("p (g c) -> p g c", g=2)[:, :, 0:64],
                           bank(b)[:, 0:128].rearrange("p (g c) -> p g c", g=2), [], [PK(b), ("avst", os_)])
                    items = []
                    for j in range(4):
                        items.append((bank(KVF)[:, j * 128:(j + 1) * 128], kf[ks][:, j * 128:(j + 1) * 128],
                                      v_st[os_][:, t, j * 128:(j + 1) * 128], True, True))
                    for j in range(4):
                        items.append((bank(KVB)[:, j * 128:(j + 1) * 128], kb[ks][:, j * 128:(j + 1) * 128],
                                      v_st[os_][:, t, j * 128:(j + 1) * 128], True, True))
                    O.mms(items, [("kf", ks), ("kb", ks), ("vst", os_, t)], [PK(KVF), PK(KVB)])
                    scan_step(st_f, rf_st[:, n, :], cdf, bank(KVF), [PK(KVF)], "stf", "rfst", "cdf")
                    O.copy("act", kvb[:, n, :], bank(KVB), [], [PK(KVB), "kvb"])
                    yield

                def st(dst, src, rkeys, tag):
                    O.dma("pool", dst, src, rkeys, [], tag)
                st(QT[G], fl(qT_st[os_][:]), [("qTst", os_, j) for j in range(4)], "stq%d" % os_)
                st(KT[G], fl(kT_st[os_][:]), [("kTst", os_, j) for j in range(4)], "stk%d" % os_)
                st(AQT[G], fl(aq_st[os_][:]), [("aqst", os_, j) for j in range(4)], "staq%d" % os_)
                st(VV[G], fl(v_st[os_][:]), [("vst", os_, t) for t in range(4)], "stv%d" % os_)
                st(GGS[G], fl(gg_st[os_][:]), [("ggst", os_)], "stg%d" % os_)
                st(AKT[sq][:, g * 512:(g + 1) * 512], ak_st[os_][:], [("akst", os_)], "stak%d" % os_)
                st(AV1[sq, g * 4:(g + 1) * 4].rearrange("t p c -> p t c"), av_st[os_][:],
                   [("avst", os_)], "stav%d" % os_)

            def run_ilA(gens):
                gens = list(gens)
                while gens:
                    for g_ in list(gens):
                        try:
                            next(g_)
                        except StopIteration:
                            gens.remove(g_)

            run_ilA([chain_X(0)])
            for sq in range(NSEQ):
                O.memsets([(st_f[:], 0.0)], ["stf"])
                for g in range(NG):
                    gens = [group_A(sq, g)]
                    if sq * NG + g + 1 < NGT:
                        gens.append(chain_X(sq * NG + g + 1))
                    run_ilA(gens)
                O.memsets([(st_b[:], 0.0)], ["stb"])
                for n in range(NT - 1, -1, -1):
                    scan_step(st_b, rb_st[:, n, :], cdb, kvb[:, n, :], ["kvb"], "stb", "rbst", "cdb")
                O.dma("pool", RF[sq], fl(rf_st[:]), ["rfst"], [], "strf")
                O.dma("pool", RB[sq], fl(rb_st[:]), ["rbst"], [], "strb")

        P.barrier()

        with contextlib.ExitStack() as pb:
            wo = sbuf(pb, "wo", [128, 8, D], BF16)
            lnp = sbuf(pb, "lnpB", [128, 2, D], F32)
            attb = sbuf(pb, "attb", [128, 2, 3, 2, 512], BF16)
            identb = sbuf(pb, "identb", [128, 128], BF16)
            maskT = sbuf(pb, "maskT", [128, 2, 4, 128], F32)
            Gf = sbuf(pb, "Gf", [128, 4, 128], F32)
            Gb = sbuf(pb, "Gb", [128, 4, 128], F32)
            esk = sbuf(pb, "esk", [128, 8], F32)
            eskr = sbuf(pb, "eskr", [128, 8], F32)
            nhalf = sbuf(pb, "nhalf", [128, 8], F32)
            gbcol = sbuf(pb, "gbcol", [128, 16], F32)
            mtmp = [sbuf(pb, "mtmp%d" % i, [128, 128], F32) for i in range(2)]
            qTb = [sbuf(pb, "qTb%d" % i, [128, 4, 512], BF16) for i in range(2)]
            kTb = [sbuf(pb, "kTb%d" % i, [128, 4, 512], BF16) for i in range(2)]
            aqb = [sbuf(pb, "aqb%d" % i, [128, 4, 512], BF16) for i in range(2)]
            vb = [sbuf(pb, "vb%d" % i, [128, 4, 512], BF16) for i in range(2)]
            ggb = [sbuf(pb, "ggb%d" % i, [128, 4, 512], BF16) for i in range(2)]
            rfb = [sbuf(pb, "rfb%d" % i, [128, 4, 512], BF16) for i in range(2)]
            rbb = [sbuf(pb, "rbb%d" % i, [128, 4, 512], BF16) for i in range(2)]
            akp = [[[sbuf(pb, "akp%d%d%d" % (i, gk, pp), [128, 768], BF16) for pp in range(2)] for gk in range(2)] for i in range(2)]
            avb = [sbuf(pb, "avb%d" % i, [128, 6, 130], BF16) for i in range(2)]
            xsb = [sbuf(pb, "xsB%d" % i, [128, D], F32) for i in range(2)]
            Sbuf = sbuf(pb, "Sbuf", [128, 2, 4, 128], BF16)
            qf = sbuf(pb, "qf", [128, 4, 128], BF16)
            qb = sbuf(pb, "qb", [128, 4, 128], BF16)
            ysq = sbuf(pb, "ysq", [128, 512], F32)
            gst = sbuf(pb, "gst", [128, 64], F32)
            qfg = [sbuf(pb, "qfg%d" % i, [128, 4, 512], BF16) for i in range(2)]
            qbg = [sbuf(pb, "qbg%d" % i, [128, 4, 512], BF16) for i in range(2)]
            yr = sbuf(pb, "yr", [128, 512], F32)
            Pb = sbuf(pb, "Pb", [128, 3, 8, 128], BF16)
            den = sbuf(pb, "den", [128, 16], F32)
            ya = sbuf(pb, "ya", [128, 512], F32)
            catT = [sbuf(pb, "catT%d" % i, [128, 8, 128], BF16) for i in range(2)]
            r_t = sbuf(pb, "r_t", [128, D], F32)
            h_t = [sbuf(pb, "h_t%d" % i, [128, D], F32) for i in range(2)]
            hT_st = [sbuf(pb, "hTst%d" % i, [128, 8, 512], BF16) for i in range(2)]
            st6 = sbuf(pb, "st6B", [128, 12], F32)
            mv = sbuf(pb, "mvB", [128, 8], F32)

            w_out_v = w_out.rearrange("(kc p) f -> p kc f", p=128)
            WOK = [("wo", 0), ("wo", 4)]
            for kc in range(0, 8, 4):
                O.dma("pool", wo[:, kc:kc + 4, :], w_out_v[:, kc:kc + 4, :], [], [("wo", kc)], "wo")
            O.dma("pool", attb[:].rearrange("p a b c d -> p (a b c d)"), attb_d.rearrange("p a b c d -> p (a b c d)"), [], ["attb"], "cb2")
            O.dma("sp", lnp[:, 0, :], ln1g[0].partition_broadcast(128), [], [("lnpx", 0)], "cb0")
            O.dma("sp", lnp[:, 1, :], ln1b[0].partition_broadcast(128), [], [("lnpx", 1)], "cb1")
            O.dma("sp", eskr[:], sink[0].partition_broadcast(128), [], ["eskr"], "cb3")
            O.dma("sp", gbcol[:, 0:8], ln1g[0].rearrange("(kc p) -> p kc", p=128), [], [("gbcol", 0)], "cb4", slow=True)
            O.dma("sp", gbcol[:, 8:16], ln1b[0].rearrange("(kc p) -> p kc", p=128), [], [("gbcol", 1)], "cb5", slow=True)
            O.act(esk[:], eskr[:], AF.Exp, ["eskr"], ["esk"])
            O.copy("dve", identb[:], ident, ["tabs"], ["identb"])
            O.memsets([(nhalf[:], -0.5)], ["nhalf"])
            for h in range(8):
                j, pp = h // 2, h % 2
                O.act(mtmp[0][:], tabs[:, 0, :], AF.Exp, ["lg", "tabs", "cst"], ["mtmp0"], scale=lg[:, h:h + 1], bias=cst[:, 2:3])
                O.act(mtmp[1][:], tabs[:, 1, :], AF.Exp, ["lg", "tabs", "cst"], ["mtmp1"], scale=lg[:, 8 + h:9 + h], bias=cst[:, 2:3])
                O.tt(maskT[:, pp, j, :], mtmp[0][:], mtmp[1][:], ALU.add, ["mtmp0", "mtmp1"], ["maskT"])
            for j in range(4):
                O.act(Gf[:, j, :], tabs[:, 2, :], AF.Exp, ["lg", "tabs"], ["Gf"], scale=lg[:, 16 + j:17 + j])
                O.act(Gb[:, j, :], tabs[:, 3, :], AF.Exp, ["lg", "tabs"], ["Gb"], scale=lg[:, 20 + j:21 + j])
            AKPK = lambda i: [("akp", i, gk, pp) for gk in range(2) for pp in range(2)]
            O.memsets([(akp[i][gk][pp][:], 0.0) for i in range(2) for gk in range(2) for pp in range(2)], AKPK(0) + AKPK(1))
            LNPK = [("lnpx", 0), ("lnpx", 1)]

            def load_group_B(G):
                sq, g = divmod(G, NG)
                bs = G % 2
                ld = lambda dst, src, key, tag: O.dma("sp", dst, src, [], [key], tag)
                ld(fl(qTb[bs][:]), QT[G], ("qTb", bs), "lq%d" % bs)
                ld(fl(kTb[bs][:]), KT[G], ("kTb", bs), "lk%d" % bs)
                ld(fl(rfb[bs][:]), RF[sq][:, g * 2048:(g + 1) * 2048], ("rfb", bs), "lrf%d" % bs)
                ld(fl(rbb[bs][:]), RB[sq][:, g * 2048:(g + 1) * 2048], ("rbb", bs), "lrb%d" % bs)
                ld(fl(vb[bs][:]), VV[G], ("vb", bs), "lv%d" % bs)
                ld(fl(aqb[bs][:]), AQT[G], ("aqb", bs), "laq%d" % bs)
                lo = max(0, g * 512 - 128)
                hi = min(S, g * 512 + 640)
                off = lo - (g * 512 - 128)
                for gk in range(2):
                    for pp in range(2):
                        ld(akp[bs][gk][pp][pp * 64:(pp + 1) * 64, off:off + (hi - lo)], AKT[sq][gk * 64:(gk + 1) * 64, lo:hi],
                           ("akp", bs, gk, pp), "lak%d%d%d" % (bs, gk, pp))
                tlo = max(0, g * 4 - 1)
                thi = min(NT, g * 4 + 5)
                toff = tlo - (g * 4 - 1)
                ld(avb[bs][:, toff:toff + (thi - tlo), :], AV1[sq, tlo:thi].rearrange("t p c -> p t c"), ("avb", bs), "lav%d" % bs)
                ld(fl(ggb[bs][:]), GGS[G], ("ggb", bs), "lg%d" % bs)
                q4 = lambda a: a.rearrange("p j (t c) -> p j t c", t=4)
                O.tt(q4(qfg[bs][:]), q4(qTb[bs][:]), Gf[:].unsqueeze(2).to_broadcast([128, 4, 4, 128]), ALU.mult,
                     [("qTb", bs), "Gf"], [("qfg", bs)])
                O.tt(q4(qbg[bs][:]), q4(qTb[bs][:]), Gb[:].unsqueeze(2).to_broadcast([128, 4, 4, 128]), ALU.mult,
                     [("qTb", bs), "Gb"], [("qbg", bs)])

            def chain_R(G, t):
                sq, g = divmod(G, NG)
                bs = G % 2
                gt = G * 4 + t
                cs_ = gt % 2
                xsl = gt % 2
                c0, c1 = t * 128, (t + 1) * 128
                O.dma("sp", xsb[xsl][:], x[gt * 128:(gt + 1) * 128, :], [], [("xsB", xsl)], "xsB%d" % xsl)
                items = []
                for j in range(4):
                    for pp in range(2):
                        items.append((bank(2 + pp)[:, j * 128:(j + 1) * 128], kTb[bs][pp * 64:(pp + 1) * 64, j, c0:c1],
                                      qTb[bs][pp * 64:(pp + 1) * 64, j, c0:c1], True, True))
                O.mms(items, [("kTb", bs), ("qTb", bs)], [PK(2), PK(3)])
                yield
                for pp in range(2):
                    O.tt(fl(Sbuf[:, pp, :, :]), bank(2 + pp), fl(maskT[:, pp, :, :]), ALU.mult, ["maskT"], [PK(2 + pp), ("Sbuf", pp)])
                    yield
                items = []
                for j in range(4):
                    yo = bank(2)[:, j * 128:(j + 1) * 128]
                    items.append((yo, qfg[bs][:, j, c0:c1], rfb[bs][:, t, j * 128:(j + 1) * 128], True, False))
                    items.append((yo, qbg[bs][:, j, c0:c1], rbb[bs][:, t, j * 128:(j + 1) * 128], False, True))
                    items.append((bank(2)[:, j * 128:j * 128 + 64], Sbuf[:, 0, j, :], vb[bs][:, t, j * 128:j * 128 + 64], False, True, True))
                    items.append((bank(2)[:, j * 128 + 64:(j + 1) * 128], Sbuf[:, 1, j, :], vb[bs][:, t, j * 128 + 64:(j + 1) * 128],
                                  False, True, True))
                O.mms(items, [("qfg", bs), ("qbg", bs), ("rfb", bs), ("rbb", bs), ("Sbuf", 0), ("Sbuf", 1), ("vb", bs)], [PK(2)])
                yield
                y3 = h8(bank(2))
                O.reduce(gst[:, 0:8], y3, ALU.add, [], [PK(2), "gs1"])
                yield
                O.act(ysq[:], bank(2), AF.Square, [], [PK(2), "ysq"])
                yield
                O.reduce(gst[:, 8:16], h8(ysq[:]), ALU.add, ["ysq"], ["gs2"])
                yield
                O.tt(gst[:, 16:24], gst[:, 0:8], gst[:, 0:8], ALU.mult, ["gs1"], ["gt"])
                yield
                O.stt(gst[:, 24:32], gst[:, 8:16], 64.0, gst[:, 16:24], ALU.mult, ALU.subtract, ["gs2", "gt"], ["gu"])
                yield
                O.ts(gst[:, 32:40], gst[:, 24:32], 1.0 / 4096.0, GN_EPS, ALU.mult, ALU.add, ["gu"], ["gve"])
                yield
                O.act(gst[:, 56:64], gst[:, 32:40], AF.Ln, ["gve"], ["glnve"])
                yield
                O.act(gst[:, 40:48], gst[:, 56:64], AF.Exp, ["glnve"], ["grstd"], scale=-0.5)
                yield
                O.stt(gst[:, 48:56], gst[:, 0:8], -1.0 / 64.0, gst[:, 40:48], ALU.mult, ALU.mult, ["gs1", "grstd"], ["gnmr"])
                yield
                O.tt(h8(yr[:]), y3, gst[:, 40:48].unsqueeze(2).to_broadcast([128, 8, 64]), ALU.mult, ["grstd"], [PK(2), "yr"])
                yield
                O.tt(h8(yr[:]), h8(yr[:]), gst[:, 48:56].unsqueeze(2).to_broadcast([128, 8, 64]), ALU.add, ["yr", "gnmr"], ["yr"])
                yield
                O.tt(yr[:], yr[:], ggb[bs][:, t, :], ALU.mult, ["yr", ("ggb", bs)], ["yr"])
                yield
                O.trs([(bank(3)[:, q * 128:(q + 1) * 128], yr[:, q * 128:(q + 1) * 128]) for q in range(4)], ident, ["yr", "tabs"], [PK(3)])
                yield
                O.copy("act", catT[cs_][:, 0:4, :], q4v(bank(3)), [], [PK(3), ("catT", cs_, 0)])
                yield

            def chain_A(G, t):
                sq, g = divmod(G, NG)
                bs = G % 2
                n = g * 4 + t
                gt = G * 4 + t
                cs_ = gt % 2
                c0, c1 = t * 128, (t + 1) * 128
                kbs = [kb_ for kb_ in range(3) if 0 <= n + kb_ - 1 < NT]
                si = 0
                for gk in range(2):
                    for kb_ in kbs:
                        sb_ = si % 2
                        si += 1
                        kc0 = (t + kb_) * 128
                        items = [(bank(sb_), identb[:], attb[:, 0, kb_, gk, :], True, False),
                                 (bank(sb_), identb[:], attb[:, 1, kb_, gk, :], False, True)]
                        for pp in range(2):
                            items.append((bank(sb_)[:, pp * 256:(pp + 1) * 256], akp[bs][gk][pp][:, kc0:kc0 + 128],
                                          aqb[bs][:, 2 * gk:2 * gk + 2, c0:c1], False, True, True))
                        O.mms(items, AKPK(bs) + [("aqb", bs), "attb", "identb"], [PK(sb_)])
                        yield
                        yield
                        pview = Pb[:, kb_, 4 * gk:4 * gk + 4, :].rearrange("p (bi pp) q -> p pp bi q", pp=2)
                        sview = bank(sb_).rearrange("p (pp bi q) -> p pp bi q", pp=2, bi=2)
                        O.act(pview, sview, AF.Exp, [], [PK(sb_), ("Pb", gk)])
                        yield
                    items = []
                    for hh in range(4):
                        h = 4 * gk + hh
                        for i_, kb_ in enumerate(kbs):
                            items.append((bank(4 + gk)[:, hh * 65:(hh + 1) * 65], Pb[:, kb_, h, :], avb[bs][:, t + kb_, gk * 65:(gk + 1) * 65],
                                          i_ == 0, i_ == len(kbs) - 1))
                    O.mms(items, [("Pb", gk), ("avb", bs)], [PK(4 + gk)])
                    yield
                    pv3 = bank(4 + gk)[:, 0:260].rearrange("p (h c) -> p h c", h=4)
                    O.tt(den[:, gk * 4:(gk + 1) * 4].unsqueeze(2), pv3[:, :, 64:65], esk[:, gk * 4:(gk + 1) * 4].unsqueeze(2), ALU.add,
                         ["esk"], [PK(4 + gk), ("den", gk)])
                    yield
                    O.recip(den[:, 8 + gk * 4:8 + (gk + 1) * 4], den[:, gk * 4:(gk + 1) * 4], [("den", gk)], [("rden", gk)])
                    yield
                    O.tt(ya[:, gk * 256:(gk + 1) * 256].rearrange("p (h c) -> p h c", h=4), pv3[:, :, 0:64],
                         den[:, 8 + gk * 4:8 + (gk + 1) * 4].unsqueeze(2).to_broadcast([128, 4, 64]), ALU.mult,
                         [("rden", gk)], [PK(4 + gk), ("ya", gk)])
                    yield
                O.trs([(bank(0)[:, q * 128:(q + 1) * 128], ya[:, q * 128:(q + 1) * 128]) for q in range(4)], ident,
                      [("ya", 0), ("ya", 1), "tabs"], [PK(0)])
                yield
                O.copy("act", catT[cs_][:, 4:8, :], q4v(bank(0)), [], [PK(0), ("catT", cs_, 1)])
                yield

            def chain_O(G, t):
                gt = G * 4 + t
                cs_ = gt % 2
                xsl = gt % 2
                hsl = gt % 2
                hs = G % 2
                c0, c1 = t * 128, (t + 1) * 128
                for nb in range(2):
                    O.acc(bank(6 + nb), [(catT[cs_][:, kc, :], wo[:, kc, nb * 512:(nb + 1) * 512]) for kc in range(8)],
                          [("catT", cs_, 0), ("catT", cs_, 1)] + WOK, [PK(6 + nb)])
                    yield
                    yield
                    yield
                    O.stt(r_t[:, nb * 512:(nb + 1) * 512], xsb[xsl][:, nb * 512:(nb + 1) * 512], ALPHA, bank(6 + nb), ALU.mult, ALU.add,
                          [("xsB", xsl)], [PK(6 + nb), ("r_t", nb)])
                    yield
                rk = [("r_t", 0), ("r_t", 1)]
                O.bnstats([(st6[:, 0:6], r_t[:, 0:512]), (st6[:, 6:12], r_t[:, 512:1024])], rk, ["Bst"])
                yield
                O.bnaggr(mv[:, 0:2], st6[:, 0:12], ["Bst"], ["Bmv"])
                yield
                O.ts(mv[:, 2:3], mv[:, 1:2], LN_EPS, None, ALU.add, None, ["Bmv"], ["Bve"])
                yield
                O.act(mv[:, 5:6], mv[:, 2:3], AF.Ln, ["Bve"], ["Blnve"])
                yield
                O.act(mv[:, 3:4], mv[:, 5:6], AF.Exp, ["Blnve"], ["Brs"], scale=-0.5)
                yield
                O.stt(mv[:, 4:5], mv[:, 0:1], -1.0, mv[:, 3:4], ALU.mult, ALU.mult, ["Bmv", "Brs"], ["Bnm"])
                yield
                hk = ("h_t", hsl)
                O.act(h_t[hsl][:], r_t[:], AF.Identity, rk + ["Brs", "Bnm"], [hk], scale=mv[:, 3:4], bias=mv[:, 4:5])
                yield
                for hb in range(2):
                    O.trs([(bank(6 + hb)[:, q * 128:(q + 1) * 128], h_t[hsl][:, (hb * 4 + q) * 128:(hb * 4 + q + 1) * 128]) for q in range(4)],
                          ident, [hk, "tabs"], [PK(6 + hb)])
                    yield
                for hb in range(2):
                    O.acts_sb([(hT_st[hs][:, hb * 4 + q, c0:c1], bank(6 + hb)[:, q * 128:(q + 1) * 128], AF.Identity,
                                gbcol[:, hb * 4 + q:hb * 4 + q + 1], gbcol[:, 8 + hb * 4 + q:8 + hb * 4 + q + 1]) for q in range(4)],
                              [("gbcol", 0), ("gbcol", 1)], [PK(6 + hb), ("hTst", hs)])
                    yield
                O.tt(h_t[hsl][:], h_t[hsl][:], lnp[:, 0, :], ALU.mult, LNPK, [hk])
                yield
                O.tt(h_t[hsl][:], h_t[hsl][:], lnp[:, 1, :], ALU.add, [hk] + LNPK, [hk])
                yield
                O.dma("pool", H32[gt], h_t[hsl][:], [hk], [], "sth%d" % hsl)
                if t == 3:
                    O.dma("pool", HT[G], fl(hT_st[hs][:]), [("hTst", hs)], [], "stht%d" % hs)

            def run_interleaved(gens):
                gens = list(gens)
                while gens:
                    for g_ in list(gens):
                        try:
                            next(g_)
                        except StopIteration:
                            gens.remove(g_)

            load_group_B(0)
            prev = None
            for G in range(NGT):
                if G + 1 < NGT:
                    load_group_B(G + 1)
                for t in range(4):
                    gens = [chain_R(G, t), chain_A(G, t)]
                    if prev is not None:
                        gens.append(chain_O(*prev))
                    run_interleaved(gens)
                    prev = (G, t)
            run_interleaved([chain_O(*prev)])

        P.barrier()

        pcd = top.enter_context(contextlib.ExitStack())
        wd = sbuf(pcd, "wd", [128, NJB, D], BF16)
        wpg = sbuf(pcd, "wpg", [128, 8, D], BF16)
        with contextlib.ExitStack() as pc:
            wg = sbuf(pc, "wg", [128, 8, FF], BF16)
            wu = sbuf(pc, "wu", [128, 8, FF], BF16)
            hTc = [sbuf(pc, "hTc%d" % i, [128, 8, 512], BF16) for i in range(2)]
            act_st = [sbuf(pc, "actst%d" % i, [128, NJB // 2, 512], BF16) for i in range(2)]
            sgc = [sbuf(pc, "sgc%d" % i, [128, 512], F32) for i in range(3)]
            w_g_v = w_g.rearrange("(kc p) f -> p kc f", p=128)
            w_u_v = w_u.rearrange("(kc p) f -> p kc f", p=128)
            NCH = (NJB + 3) // 4
            for c in range(NCH):
                cs = slice(c * 512, min(FF, (c + 1) * 512))
                O.dma("pool", wg[:, :, cs], w_g_v[:, :, cs], [], [("wg", c)], "wg%d" % c)
                O.dma("pool", wu[:, :, cs], w_u_v[:, :, cs], [], [("wu", c)], "wu%d" % c)
            w_d_v = w_d.rearrange("(jb p) f -> p jb f", p=128)
            for j0 in range(0, NJB, 2):
                O.dma("pool", wd[:, j0:j0 + 2, :], w_d_v[:, j0:j0 + 2, :], [], [("wd", j0)], "wd")
            w_pg_v = w_pg.rearrange("(kc p) f -> p kc f", p=128)
            for kc in range(0, 8, 4):
                O.dma("pool", wpg[:, kc:kc + 4, :], w_pg_v[:, kc:kc + 4, :], [], [("wpg", kc)], "wpg")

            def load_C1(G):
                sl = G % 2
                O.dma("sp", fl(hTc[sl][:]), HT[G], [], [("hTc", sl)], "lht%d" % sl)

            cc = {"pi": 0}

            def group_C1(G):
                sl = G % 2
                hlf = NJB // 2
                for jb in range(NJB):
                    pi = cc["pi"]
                    cc["pi"] += 1
                    ba, bb = (pi % 4) * 2, (pi % 4) * 2 + 1
                    ssl = pi % 3
                    O.acc(bank(ba), [(wg[:, kc, jb * 128:(jb + 1) * 128], hTc[sl][:, kc, :]) for kc in range(8)], [("wg", jb // 4), ("hTc", sl)], [PK(ba)])
                    O.acc(bank(bb), [(wu[:, kc, jb * 128:(jb + 1) * 128], hTc[sl][:, kc, :]) for kc in range(8)], [("wu", jb // 4), ("hTc", sl)], [PK(bb)])
                    O.act(sgc[ssl][:], bank(ba), AF.Silu, [], [PK(ba), ("sgc", ssl)])
                    hh_ = jb // hlf
                    O.tt(act_st[hh_][:, jb - hh_ * hlf, :], sgc[ssl][:], bank(bb), ALU.mult, [("sgc", ssl)], [PK(bb), ("actst", hh_)])
                    if jb == hlf - 1:
                        O.dma("pool", ACTS[G][:, 0:hlf * 512], fl(act_st[0][:]), [("actst", 0)], [], "sta0")
                O.dma("pool", ACTS[G][:, hlf * 512:NJB * 512], fl(act_st[1][:]), [("actst", 1)], [], "sta1")

            load_C1(0)
            for G in range(NGT):
                if G + 1 < NGT:
                    load_C1(G + 1)
                group_C1(G)

        P.barrier()

        with contextlib.ExitStack() as pd_:
            wpe = sbuf(pd_, "wpe", [128, 2, D], BF16)
            lnp2 = sbuf(pd_, "lnp2", [128, 2, D], F32)
            actc = [sbuf(pd_, "actc%d" % i, [128, NJB, 512], BF16) for i in range(2)]
            hTd = [sbuf(pd_, "hTd%d" % i, [128, 8, 512], BF16) for i in range(2)]
            h32 = [sbuf(pd_, "h32%d" % i, [128, D], F32) for i in range(2)]
            pt = [sbuf(pd_, "pt%d" % i, [128, PLE], F32) for i in range(2)]
            pT = [sbuf(pd_, "pT%d" % i, [128, 2, 128], BF16) for i in range(2)]
            sig = sbuf(pd_, "sig", [128, D], F32)
            z_t = [sbuf(pd_, "z_t%d" % i, [128, D], F32) for i in range(2)]
            nhalf2 = sbuf(pd_, "nhalf2", [128, 1], F32)
            o_t = [sbuf(pd_, "o_t%d" % i, [128, D], F32) for i in range(2)]
            st6 = sbuf(pd_, "st6D", [128, 12], F32)
            mv = sbuf(pd_, "mvD", [128, 8], F32)
            sdv = sbuf(pd_, "sdD", [128, 1], F32)
            rsv = sbuf(pd_, "rsD", [128, 1], F32)

            WDK = []
            WPGK = []
            O.dma("pool", wpe[:], w_pe.rearrange("(c p) f -> p c f", p=128), [], ["wpe"], "wpe")
            O.dma("sp", lnp2[:, 0, :], ln2g[0].partition_broadcast(128), [], [("lnpx", 0)], "cd0")
            O.dma("sp", lnp2[:, 1, :], ln2b[0].partition_broadcast(128), [], [("lnpx", 1)], "cd1")
            LNPK2 = [("lnpx", 0), ("lnpx", 1)]
            O.memsets([(nhalf2[:], -0.5)], ["nhalf2"])
            hlf = NJB // 2

            def load_C2(G):
                sl = G % 2
                O.dma("sp", fl(actc[sl][:, 0:hlf, :]), ACTS[G][:, 0:hlf * 512], [], [("actc", sl, 0)], "lac%d0" % sl)
                O.dma("sp", fl(actc[sl][:, hlf:NJB, :]), ACTS[G][:, hlf * 512:NJB * 512], [], [("actc", sl, 1)], "lac%d1" % sl)
                O.dma("sp", fl(hTd[sl][:]), HT[G], [], [("hTd", sl)], "lhd%d" % sl)

            def load_tile_C2(gt):
                sl = gt % 2
                O.dma("sp", h32[sl][:], H32[gt], [], [("h32", sl)], "lh32%d" % sl)
                O.dma("sp", pt[sl][:], pin[gt * 128:(gt + 1) * 128, :], [], [("pt", sl)], "lpt%d" % sl)

            def chain_M(G, t):
                sl = G % 2
                gt = G * 4 + t
                tsl = gt % 2
                zs = gt % 2
                c0, c1 = t * 128, (t + 1) * 128
                O.trs([(bank(6)[:, c * 128:(c + 1) * 128], pt[tsl][:, c * 128:(c + 1) * 128]) for c in range(2)], ident,
                      [("pt", tsl), "tabs"], [PK(6)])
                yield
                O.copy("act", pT[tsl][:], bank(6)[:, 0:256].rearrange("p (a b) -> p a b", a=2), [], [PK(6), ("pT", tsl)])
                yield
                for nb in range(2):
                    cs = slice(nb * 512, (nb + 1) * 512)
                    O.acc(bank(2 + nb), [(hTd[sl][:, kc, c0:c1], wpg[:, kc, cs]) for kc in range(8)], [("hTd", sl)] + WPGK, [PK(2 + nb)])
                    yield
                    O.acc(bank(4 + nb), [(pT[tsl][:, c, :], wpe[:, c, cs]) for c in range(2)], [("pT", tsl), "wpe"], [PK(4 + nb)])
                    yield
                    O.acc(bank(nb), [(actc[sl][:, jb, c0:c1], wd[:, jb, cs]) for jb in range(NJB)],
                          [("actc", sl, 0), ("actc", sl, 1)] + WDK, [PK(nb)])
                    yield
                for nb in range(2):
                    cs = slice(nb * 512, (nb + 1) * 512)
                    O.act(sig[:, cs], bank(2 + nb), AF.Sigmoid, [], [PK(2 + nb), ("sig", nb)])
                    yield
                    O.tt(sig[:, cs], sig[:, cs], bank(4 + nb), ALU.mult, [("sig", nb)], [PK(4 + nb), ("sig", nb)])
                    yield
                    O.stt(z_t[zs][:, cs], h32[tsl][:, cs], ALPHA, bank(nb), ALU.mult, ALU.add, [("h32", tsl)], [PK(nb), ("z_t", zs, nb)])
                    yield
                    O.tt(z_t[zs][:, cs], z_t[zs][:, cs], sig[:, cs], ALU.add, [("z_t", zs, nb), ("sig", nb)], [("z_t", zs, nb)])
                    yield

            def chain_L(G, t):
                gt = G * 4 + t
                tsl = gt % 2
                zs = gt % 2
                zk = [("z_t", zs, 0), ("z_t", zs, 1)]
                z = z_t[zs]
                O.bnstats([(st6[:, 0:6], z[:, 0:512]), (st6[:, 6:12], z[:, 512:1024])], zk, ["Dst"])
                yield
                O.bnaggr(mv[:, 0:2], st6[:, 0:12], ["Dst"], ["Dmv"])
                yield
                O.ts(mv[:, 2:3], mv[:, 1:2], LN_EPS, None, ALU.add, None, ["Dmv"], ["Dve"])
                yield
                O.tt(mv[:, 3:4], mv[:, 2:3], nhalf2[:, 0:1], ALU.pow, ["Dve", "nhalf2"], ["Drs"], eng="pool")
                yield
                O.stt(mv[:, 4:5], mv[:, 0:1], -1.0, mv[:, 3:4], ALU.mult, ALU.mult, ["Dmv", "Drs"], ["Dnm"])
                yield
                ok = ("o_t", tsl)
                O.act(o_t[tsl][:], z[:], AF.Identity, zk + ["Drs", "Dnm"], [ok], scale=mv[:, 3:4], bias=mv[:, 4:5])
                yield
                O.tt(o_t[tsl][:], o_t[tsl][:], lnp2[:, 0, :], ALU.mult, [ok] + LNPK2, [ok])
                yield
                O.tt(o_t[tsl][:], o_t[tsl][:], lnp2[:, 1, :], ALU.add, [ok] + LNPK2, [ok])
                yield
                fo = O.dma("pool", out[gt * 128:(gt + 1) * 128, :], o_t[tsl][:], [ok], [], "sto%d" % tsl)
                final_ops.append(fo)

            def run_il(gens):
                gens = list(gens)
                while gens:
                    for g_ in list(gens):
                        try:
                            next(g_)
                        except StopIteration:
                            gens.remove(g_)

            load_C2(0)
            load_tile_C2(0)
            prev = None
            for G in range(NGT):
                if G + 1 < NGT:
                    load_C2(G + 1)
                for t in range(4):
                    gt = G * 4 + t
                    if gt + 1 < NTT:
                        load_tile_C2(gt + 1)
                    gens = [chain_M(G, t)]
                    if prev is not None:
                        gens.append(chain_L(*prev))
                    run_il(gens)
                    prev = (G, t)
            run_il([chain_L(*prev)])

        P.emit(final_ops=final_ops[-2:])
    return nc


_NC_CACHE = {}


def kernel(x, p, w_in, ret_decay_fwd, ret_decay_bwd, ret_gn_gain, attn_sink, w_out,
           ln1_gain, ln1_bias, w_ffn_gate, w_ffn_up, w_ffn_down, w_ple_proj, w_ple_gate,
           ln2_gain, ln2_bias, _debug=False):
    x = np.asarray(x)
    B, S, _ = x.shape
    ncores = min(8, B)
    NSEQ = B // ncores
    key = (NSEQ, S, _debug)
    if key not in _NC_CACHE:
        _NC_CACHE[key] = build(NSEQ, S, debug=_debug)
    nc = _NC_CACHE[key]
    tabs, colv, attb = _const_tables()
    f = lambda a: np.ascontiguousarray(np.asarray(a, dtype=np.float32))
    shared = {
        "w_in": f(w_in)[0], "ret_decay_fwd": f(ret_decay_fwd), "ret_decay_bwd": f(ret_decay_bwd),
        "ret_gn_gain": f(ret_gn_gain), "attn_sink": f(attn_sink), "w_out": f(w_out)[0],
        "ln1_gain": f(ln1_gain), "ln1_bias": f(ln1_bias), "w_ffn_gate": f(w_ffn_gate)[0],
        "w_ffn_up": f(w_ffn_up)[0], "w_ffn_down": f(w_ffn_down)[0], "w_ple_proj": f(w_ple_proj)[0],
        "w_ple_gate": f(w_ple_gate)[0], "ln2_gain": f(ln2_gain), "ln2_bias": f(ln2_bias),
        "c_tabs": tabs, "c_colv": colv, "c_attb": attb,
    }
    xf = f(x).reshape(ncores, NSEQ * S, D)
    pf = f(p)[0].reshape(ncores, NSEQ * S, PLE)
    in_maps = []
    for c in range(ncores):
        m = dict(shared)
        m["x"] = xf[c]
        m["p"] = pf[c]
        in_maps.append(m)
    res = run_bass_kernel_spmd(nc, in_maps, core_ids=list(range(ncores)))
    if _debug:
        return res
    outp = np.stack([np.asarray(r["out"]) for r in res.results], axis=0)
    return outp.reshape(B, S, D).astype(np.float32)
```

```python
import contextlib
import math
import numpy as np
import concourse.bass as bass
import concourse.mybir as mybir
from concourse.bass_utils import run_bass_kernel_spmd

F32 = mybir.dt.float32
BF16 = mybir.dt.bfloat16
AF = mybir.ActivationFunctionType
ALU = mybir.AluOpType
AX = mybir.AxisListType

D = 1024
FF = 2816
NJB = FF // 128
INW = 2816
PLE = 256
ALPHA = 2.0 ** 0.25
LN_EPS = 1e-5
GN_EPS = 1e-5
LN2C = math.log(2.0)
LN8 = math.log(0.125)
BIG = 1.0e9

ENGS = ("pe", "act", "dve", "pool", "sp")


class _Op:
    __slots__ = ("eng", "fn", "deps", "dma_sem", "token", "signal", "idx", "is_dma")


class Prog:
    def __init__(self, nc):
        self.nc = nc
        self.ops = []
        self.last_w = {}
        self.readers = {}
        self.phys = {}
        self.phys_count = {}
        self.free_phys_q = {}
        self.phys_q = {}
        self.pending_barrier = {}

    def op(self, eng, fn, reads=(), writes=(), dma=None):
        o = _Op()
        o.eng = eng
        o.fn = fn
        o.is_dma = dma is not None
        o.dma_sem = dma
        o.signal = o.is_dma
        o.idx = len(self.ops)
        deps = []
        for k in reads:
            w = self.last_w.get(k)
            if w is not None:
                deps.append((w, "raw"))
        for k in writes:
            w = self.last_w.get(k)
            if w is not None:
                deps.append((w, "waw"))
            for r in self.readers.get(k, ()):
                deps.append((r, "war"))
        bar = self.pending_barrier.pop(eng, None)
        if bar is not None:
            for d in bar:
                deps.append((d, "bar"))
        od = []
        seen = set()
        for (d, kind) in deps:
            if d is o or d.idx in seen:
                continue
            if d.eng == eng and not d.is_dma and not o.is_dma:
                if kind != "raw" or eng == "pe":
                    continue
            seen.add(d.idx)
            od.append(d)
            d.signal = True
        o.deps = od
        for k in reads:
            self.readers.setdefault(k, []).append(o)
        for k in writes:
            self.last_w[k] = o
            self.readers[k] = []
        if o.is_dma:
            pid = self.phys.get(dma)
            if pid is None:
                fp = self.free_phys_q.setdefault(eng, [])
                pid = fp.pop() if fp else len(self.phys_count)
                self.phys[dma] = pid
                self.phys_q[pid] = eng
                self.phys_count.setdefault(pid, 0)
            c = self.phys_count[pid] + 16
            self.phys_count[pid] = c
            o.dma_sem = pid
            o.token = (("dma", pid), c)
        self.ops.append(o)
        return o

    def barrier(self):
        tails = {}
        for o in self.ops:
            if o.is_dma:
                tails[("dma", o.dma_sem)] = o
            else:
                tails[("eng", o.eng)] = o
        tl = list(tails.values())
        for e in ENGS:
            self.pending_barrier[e] = list(tl)
        self.last_w = {}
        self.readers = {}
        for pid in sorted(self.phys.values(), reverse=True):
            self.free_phys_q.setdefault(self.phys_q[pid], []).append(pid)
        self.phys = {}

    def emit(self, final_ops=()):
        nc = self.nc
        cnt = {e: 0 for e in ENGS}
        for o in self.ops:
            if not o.is_dma and o.signal:
                cnt[o.eng] += 1
                o.token = (("eng", o.eng), cnt[o.eng])
        sem_names = [("eng", e) for e in ENGS] + [("dma", k) for k in self.phys_count]
        with contextlib.ExitStack() as st:
            sems = {}
            for i, sn in enumerate(sem_names):
                sems[sn] = st.enter_context(nc.semaphore("s%d" % i))
            block = st.enter_context(nc.Block())
            per_eng = {e: [o for o in self.ops if o.eng == e] for e in ENGS}

            def body(e):
                def run(engh):
                    waited = {}

                    def wait(tok):
                        sn, v = tok
                        if waited.get(sn, 0) >= v:
                            return
                        waited[sn] = v
                        engh.wait_ge(sems[sn], v)

                    for o in per_eng[e]:
                        need = {}
                        for d in o.deps:
                            sn, v = d.token
                            if need.get(sn, 0) < v:
                                need[sn] = v
                        for sn, v in need.items():
                            wait((sn, v))
                        inst = o.fn(engh)
                        if o.is_dma:
                            inst.then_inc(sems[("dma", o.dma_sem)], 16)
                        elif o.signal:
                            inst.then_inc(sems[o.token[0]], 1)
                    if e == "sp":
                        for fo in final_ops:
                            wait(fo.token)
                return run

            block.tensor(body("pe"))
            block.scalar(body("act"))
            block.vector(body("dve"))
            block.gpsimd(body("pool"))
            block.sync(body("sp"))


def _const_tables():
    i = np.arange(128, dtype=np.float32)
    s = i[:, None]
    c = i[None, :]
    tf = np.where(c >= s, c - s, BIG).astype(np.float32)
    tb = np.where(s >= c, s - c, BIG).astype(np.float32)
    cp1 = np.broadcast_to(c + 1.0, (128, 128)).astype(np.float32)
    c128m = np.broadcast_to(128.0 - c, (128, 128)).astype(np.float32)
    bdt = np.where((np.arange(128)[:, None] // 64) == (np.arange(128)[None, :] // 64), 128.0, BIG).astype(np.float32)
    ident = np.eye(128, dtype=np.float32)
    tabs = np.stack([tf, tb, cp1, c128m, bdt, ident], axis=1)
    colv = np.stack([127.0 - i, i], axis=1).astype(np.float32)
    slopes = 2.0 ** (-(np.arange(8, dtype=np.float64) + 1.0))
    j = np.arange(128)[:, None]
    q = np.arange(128)[None, :]
    attb = np.zeros((3, 128, 8, 128), dtype=np.float32)
    for h in range(8):
        dl = 128 + q - j
        attb[0, :, h, :] = np.where(j >= q, -slopes[h] * dl, -1.0e30)
        attb[1, :, h, :] = -slopes[h] * np.abs(q - j)
        dr = 128 + j - q
        attb[2, :, h, :] = np.where(j <= q, -slopes[h] * dr, -1.0e30)
    import ml_dtypes
    a = attb.transpose(1, 0, 2, 3).reshape(128, 3, 2, 2, 2, 128)
    a = a.transpose(0, 1, 2, 4, 3, 5).reshape(128, 3, 2, 512)
    hi = a.astype(ml_dtypes.bfloat16).astype(np.float32)
    lo = (a - hi).astype(ml_dtypes.bfloat16).astype(np.float32)
    attb2 = np.ascontiguousarray(np.stack([hi, lo], axis=1))
    return np.ascontiguousarray(tabs), colv, attb2


class Ops:
    def __init__(self, P):
        self.P = P

    def mms(self, items, reads, writes):
        items = [tuple(it) for it in items]

        def fn(e):
            inst = None
            for it in items:
                o, l, r, s0, s1 = it[:5]
                if len(it) > 5 and it[5]:
                    inst = e.matmul(o, lhsT=l, rhs=r, start=s0, stop=s1, skip_group_check=True)
                else:
                    inst = e.matmul(o, lhsT=l, rhs=r, start=s0, stop=s1)
            return inst
        return self.P.op("pe", fn, reads, writes)

    def acc(self, out, pairs, reads, writes):
        pairs = list(pairs)
        n = len(pairs)
        return self.mms([(out, l, r, i == 0, i == n - 1) for i, (l, r) in enumerate(pairs)], reads, writes)

    def trs(self, items, ident, reads, writes):
        items = list(items)

        def fn(e):
            inst = None
            for (o, i_) in items:
                inst = e.transpose(o, i_, ident)
            return inst
        return self.P.op("pe", fn, reads, writes)

    def act(self, out, in_, func, reads, writes, scale=None, bias=None):
        kw = {}
        if scale is not None:
            kw["scale"] = scale
        if bias is not None:
            kw["bias"] = bias
        return self.P.op("act", lambda e: e.activation(out=out, in_=in_, func=func, **kw), reads, writes)

    def acts(self, items, reads, writes):
        items = list(items)

        def fn(e):
            inst = None
            for (o, i_, f_) in items:
                inst = e.activation(out=o, in_=i_, func=f_)
            return inst
        return self.P.op("act", fn, reads, writes)

    def tt(self, out, in0, in1, alu, reads, writes, eng="dve"):
        return self.P.op(eng, lambda e: e.tensor_tensor(out=out, in0=in0, in1=in1, op=alu), reads, writes)

    def stt(self, out, in0, scalar, in1, op0, op1, reads, writes):
        return self.P.op("dve", lambda e: e.scalar_tensor_tensor(out=out, in0=in0, scalar=scalar, in1=in1, op0=op0, op1=op1),
                         reads, writes)

    def ts(self, out, in0, s1, s2, op0, op1, reads, writes):
        if s2 is None:
            return self.P.op("dve", lambda e: e.tensor_scalar(out=out, in0=in0, scalar1=s1, scalar2=None, op0=op0), reads, writes)
        return self.P.op("dve", lambda e: e.tensor_scalar(out=out, in0=in0, scalar1=s1, scalar2=s2, op0=op0, op1=op1), reads, writes)

    def copy(self, eng, out, in_, reads, writes, scale=None):
        if eng == "act":
            if scale is not None:
                return self.P.op("act", lambda e: e.activation(out=out, in_=in_, func=AF.Copy, scale=scale), reads, writes)
            return self.P.op("act", lambda e: e.activation(out=out, in_=in_, func=AF.Copy), reads, writes)
        assert scale is None
        return self.P.op(eng, lambda e: e.tensor_copy(out=out, in_=in_), reads, writes)

    def memsets(self, items, writes, eng="dve"):
        items = list(items)

        def fn(e):
            inst = None
            for (ap, v) in items:
                inst = e.memset(ap, v)
            return inst
        return self.P.op(eng, fn, (), writes)

    def reduce(self, out, in_, alu, reads, writes):
        return self.P.op("dve", lambda e: e.tensor_reduce(out=out, in_=in_, axis=AX.X, op=alu), reads, writes)

    def recip(self, out, in_, reads, writes):
        return self.P.op("dve", lambda e: e.reciprocal(out=out, in_=in_), reads, writes)

    def bnstats(self, items, reads, writes):
        items = list(items)

        def fn(e):
            inst = None
            for (o, i_) in items:
                inst = e.bn_stats(out=o, in_=i_)
            return inst
        return self.P.op("dve", fn, reads, writes)

    def bnaggr(self, out, in_, reads, writes):
        return self.P.op("dve", lambda e: e.bn_aggr(out=out, in_=in_), reads, writes)

    def dma(self, eng, out, in_, reads, writes, sem, slow=False):
        if slow:
            return self.P.op(eng, lambda e: e.dma_start(out=out, in_=in_, allow_slow_non_contiguous=True), reads, writes, dma=sem)
        return self.P.op(eng, lambda e: e.dma_start(out=out, in_=in_), reads, writes, dma=sem)

    def acts_sb(self, items, reads, writes):
        items = list(items)

        def fn(e):
            inst = None
            for (o, i_, f_, sc_, bi_) in items:
                inst = e.activation(out=o, in_=i_, func=f_, scale=sc_, bias=bi_)
            return inst
        return self.P.op("act", fn, reads, writes)


def build(NSEQ, S, debug=False):
    NT = S // 128
    NG = S // 512
    TOK = NSEQ * S
    NGT = NSEQ * NG
    NTT = NSEQ * NT
    nc = bass.Bass("TRN2", target_bir_lowering=False)

    def din(name, shape, dt=F32):
        return nc.dram_tensor(name, list(shape), dt, kind="ExternalInput").ap()

    x = din("x", [TOK, D])
    pin = din("p", [TOK, PLE])
    w_in = din("w_in", [D, INW])
    dfw = din("ret_decay_fwd", [1, 8])
    dbw = din("ret_decay_bwd", [1, 8])
    gng = din("ret_gn_gain", [1, 512])
    sink = din("attn_sink", [1, 8])
    w_out = din("w_out", [D, D])
    ln1g = din("ln1_gain", [1, D])
    ln1b = din("ln1_bias", [1, D])
    w_g = din("w_ffn_gate", [D, FF])
    w_u = din("w_ffn_up", [D, FF])
    w_d = din("w_ffn_down", [FF, D])
    w_pe = din("w_ple_proj", [PLE, D])
    w_pg = din("w_ple_gate", [D, D])
    ln2g = din("ln2_gain", [1, D])
    ln2b = din("ln2_bias", [1, D])
    tabs_d = din("c_tabs", [128, 6, 128])
    colv_d = din("c_colv", [128, 2])
    attb_d = din("c_attb", [128, 2, 3, 2, 512])
    out = nc.dram_tensor("out", [TOK, D], F32, kind="ExternalOutput").ap()

    skind = "ExternalOutput" if debug else "Internal"

    def dscr(name, shape, dt):
        return nc.dram_tensor(name, list(shape), dt, kind=skind).ap()

    QT = dscr("s_qt", [NGT, 128, 4 * 512], BF16)
    KT = dscr("s_kt", [NGT, 128, 4 * 512], BF16)
    AQT = dscr("s_aqt", [NGT, 128, 4 * 512], BF16)
    VV = dscr("s_v", [NGT, 128, 4 * 512], BF16)
    GGS = dscr("s_gg", [NGT, 128, 4 * 512], BF16)
    AKT = dscr("s_akt", [NSEQ, 128, S], BF16)
    AV1 = dscr("s_av1", [NSEQ, NT, 128, 130], BF16)
    RF = dscr("s_rf", [NSEQ, 128, NT * 512], BF16)
    RB = dscr("s_rb", [NSEQ, 128, NT * 512], BF16)
    H32 = dscr("s_h32", [NTT, 128, D], F32)
    HT = dscr("s_ht", [NGT, 128, 8 * 512], BF16)
    ACTS = dscr("s_act", [NGT, 128, NJB * 512], BF16)

    P = Prog(nc)
    O = Ops(P)
    final_ops = []

    def fl(ap3):
        return ap3.rearrange("p a b -> p (a b)")

    def h8(ap2):
        return ap2.rearrange("p (h d) -> p h d", h=8)

    def q4v(ap2):
        return ap2.rearrange("p (a b) -> p a b", a=4)

    with contextlib.ExitStack() as top:
        def sbuf(stack, name, shape, dt):
            return stack.enter_context(nc.sbuf_tensor(name, list(shape), dt))

        ps = top.enter_context(nc.psum_tensor("ps", [128, 8 * 512], F32))

        def bank(i):
            return ps[:, i * 512:(i + 1) * 512]

        def PK(i):
            return ("ps", i)

        tabs = sbuf(top, "tabs", [128, 6, 128], F32)
        colv = sbuf(top, "colv", [128, 2], F32)
        dd = sbuf(top, "dd", [128, 24], F32)
        de = sbuf(top, "de", [128, 24], F32)
        lg = sbuf(top, "lg", [128, 24], F32)
        cst = sbuf(top, "cst", [128, 4], F32)
        ident = tabs[:, 5, :]

        O.dma("sp", tabs[:], tabs_d, [], ["tabs"], "c0")
        O.dma("sp", colv[:], colv_d, [], ["colv"], "c1")
        O.dma("sp", dd[:, 0:8], dfw[0].partition_broadcast(128), [], [("dd", 0)], "c2")
        O.dma("sp", dd[:, 8:16], dbw[0].partition_broadcast(128), [], [("dd", 1)], "c3")
        ddk = [("dd", 0), ("dd", 1), ("dd", 2)]
        for t in range(2):
            O.copy("dve", dd[t * 64:(t + 1) * 64, 16:20], dd[t * 64:(t + 1) * 64, t:8:2], [("dd", 0)], [("dd", 2)])
            O.copy("dve", dd[t * 64:(t + 1) * 64, 20:24], dd[t * 64:(t + 1) * 64, 8 + t:16:2], [("dd", 1)], [("dd", 2)])
        O.memsets([(cst[:, 0:1], LN_EPS), (cst[:, 1:2], GN_EPS), (cst[:, 2:3], LN8), (cst[:, 3:4], 1.0)], ["cst"])
        O.act(de[:], dd[:], AF.Exp, ddk, ["de"], scale=LN2C)
        O.act(lg[:], de[:], AF.Ln, ["de", "cst"], ["lg"], scale=-1.0, bias=cst[:, 3:4])

        def lnorm(z, zkeys, gain, bias, lnkeys, outt, okey, st6, mv, sd, rs, tag):
            k_st, k_mv, k_sd, k_rs = [tag + s_ for s_ in ("st", "mv", "sd", "rs")]
            O.bnstats([(st6[:, 0:6], z[:, 0:512]), (st6[:, 6:12], z[:, 512:1024])], zkeys, [k_st])
            O.bnaggr(mv[:, 0:2], st6[:, 0:12], [k_st], [k_mv])
            O.act(sd[:, 0:1], mv[:, 1:2], AF.Sqrt, [k_mv, "cst"], [k_sd], scale=1.0, bias=cst[:, 0:1])
            O.recip(rs[:, 0:1], sd[:, 0:1], [k_sd], [k_rs])
            O.ts(outt, z, mv[:, 0:1], rs[:, 0:1], ALU.subtract, ALU.mult, zkeys + [k_mv, k_rs], [okey])
            O.tt(outt, outt, gain, ALU.mult, [okey] + lnkeys, [okey])
            O.tt(outt, outt, bias, ALU.add, [okey] + lnkeys, [okey])

        with contextlib.ExitStack() as pa:
            win = sbuf(pa, "win", [128, 8, INW], BF16)
            xs = [sbuf(pa, "xsA%d" % i, [128, D], F32) for i in range(3)]
            xT = [sbuf(pa, "xTA%d" % i, [128, 8, 512], BF16) for i in range(2)]
            qT_st = [sbuf(pa, "qTst%d" % i, [128, 4, 512], BF16) for i in range(2)]
            kT_st = [sbuf(pa, "kTst%d" % i, [128, 4, 512], BF16) for i in range(2)]
            aq_st = [sbuf(pa, "aqst%d" % i, [128, 4, 512], BF16) for i in range(2)]
            ak_st = [sbuf(pa, "akst%d" % i, [128, 512], BF16) for i in range(2)]
            v_st = [sbuf(pa, "vst%d" % i, [128, 4, 512], BF16) for i in range(2)]
            gg_st = [sbuf(pa, "ggst%d" % i, [128, 4, 512], BF16) for i in range(2)]
            av_st = [sbuf(pa, "avst%d" % i, [128, 4, 130], BF16) for i in range(2)]
            kf = [sbuf(pa, "kf%d" % i, [128, 512], BF16) for i in range(2)]
            kb = [sbuf(pa, "kb%d" % i, [128, 512], BF16) for i in range(2)]
            sg = [sbuf(pa, "sgA%d" % i, [128, 512], F32) for i in range(2)]
            kvb = sbuf(pa, "kvb", [128, NT, 512], F32)
            rf_st = sbuf(pa, "rfst", [128, NT, 512], BF16)
            rb_st = sbuf(pa, "rbst", [128, NT, 512], BF16)
            st_f = sbuf(pa, "stf", [128, 512], F32)
            st_b = sbuf(pa, "stb", [128, 512], F32)
            cdf = sbuf(pa, "cdf", [128, 4, 128], F32)
            cdb = sbuf(pa, "cdb", [128, 4, 128], F32)
            dk = sbuf(pa, "dk", [128, 16], F32)
            gnr = sbuf(pa, "gnr", [128, 512], F32)

            w_in_v = w_in.rearrange("(kc p) f -> p kc f", p=128)
            WINK = [("win", kc) for kc in range(8)]
            for kc in range(8):
                O.dma("pool", win[:, kc, :], w_in_v[:, kc, :], [], [("win", kc)], "win")
            O.dma("sp", gnr[:], gng[0].partition_broadcast(128), [], ["gnr"], "c6")

            O.act(dk[:, 0:8], lg[:, 0:8], AF.Exp, ["lg", "colv", "cst"], [("dk", 0)], scale=colv[:, 0:1], bias=cst[:, 2:3])
            O.act(dk[:, 8:16], lg[:, 8:16], AF.Exp, ["lg", "colv", "cst"], [("dk", 1)], scale=colv[:, 1:2], bias=cst[:, 2:3])
            for j in range(4):
                O.act(cdf[:, j, :], tabs[:, 4, :], AF.Exp, ["lg", "tabs"], ["cdf"], scale=lg[:, 16 + j:17 + j])
                O.act(cdb[:, j, :], tabs[:, 4, :], AF.Exp, ["lg", "tabs"], ["cdb"], scale=lg[:, 20 + j:21 + j])
            O.memsets([(fl(rf_st[:]), 0.0), (fl(rb_st[:]), 0.0), (fl(av_st[0][:]), 1.0), (fl(av_st[1][:]), 1.0)],
                      ["rfst", "rbst", ("avst", 0), ("avst", 1)])

            BT = [0, 1]
            ACC = [2, 3, 4, 5]
            KVF, KVB = 6, 7
            cnt = {"acc": 0, "ev": 0}

            def next_acc():
                b = ACC[cnt["acc"] % 4]
                cnt["acc"] += 1
                return b

            def evac(dst, src, pkey, wkey, scale=None):
                cnt["ev"] += 1
                if scale is not None or cnt["ev"] % 2 == 0:
                    O.copy("act", dst, src, [], [pkey, wkey], scale=scale)
                else:
                    O.copy("dve", dst, src, [], [pkey, wkey])

            def scan_step(st, rst_n, cd, kv_src, kvkeys, skey, rkey, cdkey):
                sv = q4v(st[:])
                dv = q4v(rst_n)
                O.acts([(dv[0:64, :, 0:64], sv[0:64, :, 0:64], AF.Copy), (dv[64:128, :, 64:128], sv[64:128, :, 64:128], AF.Copy)],
                       [skey], [rkey])
                O.tt(st[:], st[:], fl(cd[:]), ALU.mult, [skey, cdkey], [skey])
                O.tt(st[:], st[:], kv_src, ALU.add, [skey], [skey] + kvkeys)

            def chain_X(G):
                gs = G % 2
                for t in range(4):
                    gt = G * 4 + t
                    sl = gt % 3
                    O.dma("sp", xs[sl][:], x[gt * 128:(gt + 1) * 128, :], [], [("xsA", sl)], "xsA%d" % sl)
                    for hb in range(2):
                        O.trs([(bank(BT[hb])[:, q * 128:(q + 1) * 128], xs[sl][:, (hb * 4 + q) * 128:(hb * 4 + q + 1) * 128]) for q in range(4)],
                              ident, [("xsA", sl), "tabs"], [PK(BT[hb])])
                        yield
                        evac(xT[gs][:, hb * 4:(hb + 1) * 4, t * 128:(t + 1) * 128], q4v(bank(BT[hb])), PK(BT[hb]), ("xTA", gs, t, hb))
                        yield

            def group_A(sq, g):
                G = sq * NG + g
                gs = G % 2
                os_ = G % 2
                xTk = [("xTA", gs, t, hb) for t in range(4) for hb in range(2)]

                def fm_block(col, dst, dkey, scale=None):
                    b = next_acc()
                    O.acc(bank(b), [(win[:, kc, col:col + 128], xT[gs][:, kc, :]) for kc in range(8)], WINK + xTk, [PK(b)])
                    evac(dst, bank(b), PK(b), dkey, scale=scale)

                for j in range(4):
                    fm_block(j * 128, qT_st[os_][:, j, :], ("qTst", os_, j))
                    yield
                for j in range(4):
                    fm_block(512 + j * 128, kT_st[os_][:, j, :], ("kTst", os_, j))
                    yield
                for j in range(4):
                    fm_block(2048 + j * 128, aq_st[os_][:, j, :], ("aqst", os_, j), scale=0.125)
                    yield
                fm_block(2560, ak_st[os_][:], ("akst", os_))
                yield

                for t in range(4):
                    n = g * 4 + t
                    ks = (G * 4 + t) % 2
                    xk = [("xTA", gs, t, 0), ("xTA", gs, t, 1)]

                    def tm_mm(col, width):
                        b = next_acc()
                        O.acc(bank(b)[:, 0:width], [(xT[gs][:, kc, t * 128:(t + 1) * 128], win[:, kc, col:col + width]) for kc in range(8)],
                              WINK + xk, [PK(b)])
                        return b

                    b = tm_mm(512, 512)
                    O.tt(h8(kf[ks][:]), h8(bank(b)), dk[:, 0:8].unsqueeze(2).to_broadcast([128, 8, 64]), ALU.mult,
                         [("dk", 0)], [PK(b), ("kf", ks)])
                    O.tt(h8(kb[ks][:]), h8(bank(b)), dk[:, 8:16].unsqueeze(2).to_broadcast([128, 8, 64]), ALU.mult,
                         [("dk", 1)], [PK(b), ("kb", ks)])
                    b = tm_mm(1024, 512)
                    O.copy("act", v_st[os_][:, t, :], bank(b), [], [PK(b), ("vst", os_, t)])
                    b = tm_mm(1536, 512)
                    O.act(sg[ks][:], bank(b), AF.Silu, [], [PK(b), ("sgA", ks)])
                    O.tt(gg_st[os_][:, t, :], sg[ks][:], gnr[:], ALU.mult, [("sgA", ks), "gnr"], [("ggst", os_)])
                    b = tm_mm(2688, 128)
                    O.copy("act", av_st[os_][:, t, :].rearrange("p (g c) -> p g c", g=2)[:, :, 0:64],
                           bank(b)[:, 0:128].rearrange("p (g c) -> p g c", g=2), [], [PK(b), ("avst", os_)])
                    items = []
                    for j in range(4):
                        items.append((bank(KVF)[:, j * 128:(j + 1) * 128], kf[ks][:, j * 128:(j + 1) * 128],
                                      v_st[os_][:, t, j * 128:(j + 1) * 128], True, True))
                    for j in range(4):
                        items.append((bank(KVB)[:, j * 128:(j + 1) * 128], kb[ks][:, j * 128:(j + 1) * 128],
                                      v_st[os_][:, t, j * 128:(j + 1) * 128], True, True))
                    O.mms(items, [("kf", ks), ("kb", ks), ("vst", os_, t)], [PK(KVF), PK(KVB)])
                    scan_step(st_f, rf_st[:, n, :], cdf, bank(KVF), [PK(KVF)], "stf", "rfst", "cdf")
                    O.copy("act", kvb[:, n, :], bank(KVB), [], [PK(KVB), "kvb"])
                    yield

                def st(dst, src, rkeys, tag):
                    O.dma("pool", dst, src, rkeys, [], tag)
                st(QT[G], fl(qT_st[os_][:]), [("qTst", os_, j) for j in range(4)], "stq%d" % os_)
                st(KT[G], fl(kT_st[os_][:]), [("kTst", os_, j) for j in range(4)], "stk%d" % os_)
                st(AQT[G], fl(aq_st[os_][:]), [("aqst", os_, j) for j in range(4)], "staq%d" % os_)
                st(VV[G], fl(v_st[os_][:]), [("vst", os_, t) for t in range(4)], "stv%d" % os_)
                st(GGS[G], fl(gg_st[os_][:]), [("ggst", os_)], "stg%d" % os_)
                st(AKT[sq][:, g * 512:(g + 1) * 512], ak_st[os_][:], [("akst", os_)], "stak%d" % os_)
                st(AV1[sq, g * 4:(g + 1) * 4].rearrange("t p c -> p t c"), av_st[os_][:],
                   [("avst", os_)], "stav%d" % os_)

            def run_ilA(gens):
                gens = list(gens)
                while gens:
                    for g_ in list(gens):
                        try:
                            next(g_)
                        except StopIteration:
                            gens.remove(g_)

            run_ilA([chain_X(0)])
            for sq in range(NSEQ):
                O.memsets([(st_f[:], 0.0)], ["stf"])
                for g in range(NG):
                    gens = [group_A(sq, g)]
                    if sq * NG + g + 1 < NGT:
                        gens.append(chain_X(sq * NG + g + 1))
                    run_ilA(gens)
                O.memsets([(st_b[:], 0.0)], ["stb"])
                for n in range(NT - 1, -1, -1):
                    scan_step(st_b, rb_st[:, n, :], cdb, kvb[:, n, :], ["kvb"], "stb", "rbst", "cdb")
                O.dma("pool", RF[sq], fl(rf_st[:]), ["rfst"], [], "strf")
                O.dma("pool", RB[sq], fl(rb_st[:]), ["rbst"], [], "strb")

        P.barrier()

        with contextlib.ExitStack() as pb:
            wo = sbuf(pb, "wo", [128, 8, D], BF16)
            lnp = sbuf(pb, "lnpB", [128, 2, D], F32)
            attb = sbuf(pb, "attb", [128, 2, 3, 2, 512], BF16)
            identb = sbuf(pb, "identb", [128, 128], BF16)
            maskT = sbuf(pb, "maskT", [128, 2, 4, 128], F32)
            Gf = sbuf(pb, "Gf", [128, 4, 128], F32)
            Gb = sbuf(pb, "Gb", [128, 4, 128], F32)
            esk = sbuf(pb, "esk", [128, 8], F32)
            eskr = sbuf(pb, "eskr", [128, 8], F32)
            nhalf = sbuf(pb, "nhalf", [128, 8], F32)
            gbcol = sbuf(pb, "gbcol", [128, 16], F32)
            mtmp = [sbuf(pb, "mtmp%d" % i, [128, 128], F32) for i in range(2)]
            qTb = [sbuf(pb, "qTb%d" % i, [128, 4, 512], BF16) for i in range(2)]
            kTb = [sbuf(pb, "kTb%d" % i, [128, 4, 512], BF16) for i in range(2)]
            aqb = [sbuf(pb, "aqb%d" % i, [128, 4, 512], BF16) for i in range(2)]
            vb = [sbuf(pb, "vb%d" % i, [128, 4, 512], BF16) for i in range(2)]
            ggb = [sbuf(pb, "ggb%d" % i, [128, 4, 512], BF16) for i in range(2)]
            rfb = [sbuf(pb, "rfb%d" % i, [128, 4, 512], BF16) for i in range(2)]
            rbb = [sbuf(pb, "rbb%d" % i, [128, 4, 512], BF16) for i in range(2)]
            akp = [[[sbuf(pb, "akp%d%d%d" % (i, gk, pp), [128, 768], BF16) for pp in range(2)] for gk in range(2)] for i in range(2)]
            avb = [sbuf(pb, "avb%d" % i, [128, 6, 130], BF16) for i in range(2)]
            xsb = [sbuf(pb, "xsB%d" % i, [128, D], F32) for i in range(2)]
            Sbuf = sbuf(pb, "Sbuf", [128, 2, 4, 128], BF16)
            qf = sbuf(pb, "qf", [128, 4, 128], BF16)
            qb = sbuf(pb, "qb", [128, 4, 128], BF16)
            ysq = sbuf(pb, "ysq", [128, 512], F32)
            gst = sbuf(pb, "gst", [128, 64], F32)
            qfg = [sbuf(pb, "qfg%d" % i, [128, 4, 512], BF16) for i in range(2)]
            qbg = [sbuf(pb, "qbg%d" % i, [128, 4, 512], BF16) for i in range(2)]
            yr = sbuf(pb, "yr", [128, 512], F32)
            yrb = sbuf(pb, "yrb", [128, 512], BF16)
            yab = sbuf(pb, "yab", [128, 512], BF16)
            Pb = sbuf(pb, "Pb", [128, 3, 8, 128], BF16)
            den = sbuf(pb, "den", [128, 16], F32)
            ya = sbuf(pb, "ya", [128, 512], F32)
            catT = [sbuf(pb, "catT%d" % i, [128, 8, 128], BF16) for i in range(2)]
            r_t = sbuf(pb, "r_t", [128, D], F32)
            h_t = [sbuf(pb, "h_t%d" % i, [128, D], F32) for i in range(2)]
            hT_st = [sbuf(pb, "hTst%d" % i, [128, 8, 512], BF16) for i in range(2)]
            st6 = sbuf(pb, "st6B", [128, 12], F32)
            mv = sbuf(pb, "mvB", [128, 8], F32)

            w_out_v = w_out.rearrange("(kc p) f -> p kc f", p=128)
            WOK = [("wo", 0), ("wo", 4)]
            for kc in range(0, 8, 4):
                O.dma("pool", wo[:, kc:kc + 4, :], w_out_v[:, kc:kc + 4, :], [], [("wo", kc)], "wo")
            O.dma("pool", attb[:].rearrange("p a b c d -> p (a b c d)"), attb_d.rearrange("p a b c d -> p (a b c d)"), [], ["attb"], "cb2")
            O.dma("sp", lnp[:, 0, :], ln1g[0].partition_broadcast(128), [], [("lnpx", 0)], "cb0")
            O.dma("sp", lnp[:, 1, :], ln1b[0].partition_broadcast(128), [], [("lnpx", 1)], "cb1")
            O.dma("sp", eskr[:], sink[0].partition_broadcast(128), [], ["eskr"], "cb3")
            O.act(esk[:], eskr[:], AF.Exp, ["eskr"], ["esk"])
            O.copy("dve", identb[:], ident, ["tabs"], ["identb"])
            O.memsets([(nhalf[:], -0.5)], ["nhalf"])
            for h in range(8):
                j, pp = h // 2, h % 2
                O.act(mtmp[0][:], tabs[:, 0, :], AF.Exp, ["lg", "tabs", "cst"], ["mtmp0"], scale=lg[:, h:h + 1], bias=cst[:, 2:3])
                O.act(mtmp[1][:], tabs[:, 1, :], AF.Exp, ["lg", "tabs", "cst"], ["mtmp1"], scale=lg[:, 8 + h:9 + h], bias=cst[:, 2:3])
                O.tt(maskT[:, pp, j, :], mtmp[0][:], mtmp[1][:], ALU.add, ["mtmp0", "mtmp1"], ["maskT"])
            for j in range(4):
                O.act(Gf[:, j, :], tabs[:, 2, :], AF.Exp, ["lg", "tabs"], ["Gf"], scale=lg[:, 16 + j:17 + j])
                O.act(Gb[:, j, :], tabs[:, 3, :], AF.Exp, ["lg", "tabs"], ["Gb"], scale=lg[:, 20 + j:21 + j])
            AKPK = lambda i: [("akp", i, gk, pp) for gk in range(2) for pp in range(2)]
            O.memsets([(akp[i][gk][pp][:], 0.0) for i in range(2) for gk in range(2) for pp in range(2)], AKPK(0) + AKPK(1))
            LNPK = [("lnpx", 0), ("lnpx", 1)]

            def load_group_B(G):
                sq, g = divmod(G, NG)
                bs = G % 2
                ld = lambda dst, src, key, tag: O.dma("sp", dst, src, [], [key], tag)
                ld(fl(qTb[bs][:]), QT[G], ("qTb", bs), "lq%d" % bs)
                ld(fl(kTb[bs][:]), KT[G], ("kTb", bs), "lk%d" % bs)
                ld(fl(rfb[bs][:]), RF[sq][:, g * 2048:(g + 1) * 2048], ("rfb", bs), "lrf%d" % bs)
                ld(fl(rbb[bs][:]), RB[sq][:, g * 2048:(g + 1) * 2048], ("rbb", bs), "lrb%d" % bs)
                ld(fl(vb[bs][:]), VV[G], ("vb", bs), "lv%d" % bs)
                ld(fl(aqb[bs][:]), AQT[G], ("aqb", bs), "laq%d" % bs)
                lo = max(0, g * 512 - 128)
                hi = min(S, g * 512 + 640)
                off = lo - (g * 512 - 128)
                for gk in range(2):
                    for pp in range(2):
                        ld(akp[bs][gk][pp][pp * 64:(pp + 1) * 64, off:off + (hi - lo)], AKT[sq][gk * 64:(gk + 1) * 64, lo:hi],
                           ("akp", bs, gk, pp), "lak%d%d%d" % (bs, gk, pp))
                tlo = max(0, g * 4 - 1)
                thi = min(NT, g * 4 + 5)
                toff = tlo - (g * 4 - 1)
                ld(avb[bs][:, toff:toff + (thi - tlo), :], AV1[sq, tlo:thi].rearrange("t p c -> p t c"), ("avb", bs), "lav%d" % bs)
                ld(fl(ggb[bs][:]), GGS[G], ("ggb", bs), "lg%d" % bs)
                q4 = lambda a: a.rearrange("p j (t c) -> p j t c", t=4)
                O.tt(q4(qfg[bs][:]), q4(qTb[bs][:]), Gf[:].unsqueeze(2).to_broadcast([128, 4, 4, 128]), ALU.mult,
                     [("qTb", bs), "Gf"], [("qfg", bs)])
                O.tt(q4(qbg[bs][:]), q4(qTb[bs][:]), Gb[:].unsqueeze(2).to_broadcast([128, 4, 4, 128]), ALU.mult,
                     [("qTb", bs), "Gb"], [("qbg", bs)])

            def chain_R(G, t):
                sq, g = divmod(G, NG)
                bs = G % 2
                gt = G * 4 + t
                cs_ = gt % 2
                xsl = gt % 2
                c0, c1 = t * 128, (t + 1) * 128
                O.dma("sp", xsb[xsl][:], x[gt * 128:(gt + 1) * 128, :], [], [("xsB", xsl)], "xsB%d" % xsl)
                items = []
                for j in range(4):
                    for pp in range(2):
                        items.append((bank(2 + pp)[:, j * 128:(j + 1) * 128], kTb[bs][pp * 64:(pp + 1) * 64, j, c0:c1],
                                      qTb[bs][pp * 64:(pp + 1) * 64, j, c0:c1], True, True))
                O.mms(items, [("kTb", bs), ("qTb", bs)], [PK(2), PK(3)])
                yield
                for pp in range(2):
                    O.tt(fl(Sbuf[:, pp, :, :]), bank(2 + pp), fl(maskT[:, pp, :, :]), ALU.mult, ["maskT"], [PK(2 + pp), ("Sbuf", pp)])
                    yield
                items = []
                for j in range(4):
                    yo = bank(2)[:, j * 128:(j + 1) * 128]
                    items.append((yo, qfg[bs][:, j, c0:c1], rfb[bs][:, t, j * 128:(j + 1) * 128], True, False))
                    items.append((yo, qbg[bs][:, j, c0:c1], rbb[bs][:, t, j * 128:(j + 1) * 128], False, True))
                    items.append((bank(2)[:, j * 128:j * 128 + 64], Sbuf[:, 0, j, :], vb[bs][:, t, j * 128:j * 128 + 64], False, True, True))
                    items.append((bank(2)[:, j * 128 + 64:(j + 1) * 128], Sbuf[:, 1, j, :], vb[bs][:, t, j * 128 + 64:(j + 1) * 128],
                                  False, True, True))
                O.mms(items, [("qfg", bs), ("qbg", bs), ("rfb", bs), ("rbb", bs), ("Sbuf", 0), ("Sbuf", 1), ("vb", bs)], [PK(2)])
                yield
                y3 = h8(bank(2))
                O.reduce(gst[:, 0:8], y3, ALU.add, [], [PK(2), "gs1"])
                yield
                O.act(ysq[:], bank(2), AF.Square, [], [PK(2), "ysq"])
                yield
                O.reduce(gst[:, 8:16], h8(ysq[:]), ALU.add, ["ysq"], ["gs2"])
                yield
                O.tt(gst[:, 16:24], gst[:, 0:8], gst[:, 0:8], ALU.mult, ["gs1"], ["gt"])
                yield
                O.stt(gst[:, 24:32], gst[:, 8:16], 64.0, gst[:, 16:24], ALU.mult, ALU.subtract, ["gs2", "gt"], ["gu"])
                yield
                O.ts(gst[:, 32:40], gst[:, 24:32], 1.0 / 4096.0, GN_EPS, ALU.mult, ALU.add, ["gu"], ["gve"])
                yield
                O.act(gst[:, 56:64], gst[:, 32:40], AF.Ln, ["gve"], ["glnve"])
                yield
                O.act(gst[:, 40:48], gst[:, 56:64], AF.Exp, ["glnve"], ["grstd"], scale=-0.5)
                yield
                O.stt(gst[:, 48:56], gst[:, 0:8], -1.0 / 64.0, gst[:, 40:48], ALU.mult, ALU.mult, ["gs1", "grstd"], ["gnmr"])
                yield
                O.tt(h8(yr[:]), y3, gst[:, 40:48].unsqueeze(2).to_broadcast([128, 8, 64]), ALU.mult, ["grstd"], [PK(2), "yr"])
                yield
                O.tt(h8(yr[:]), h8(yr[:]), gst[:, 48:56].unsqueeze(2).to_broadcast([128, 8, 64]), ALU.add, ["yr", "gnmr"], ["yr"])
                yield
                O.tt(yrb[:], yr[:], ggb[bs][:, t, :], ALU.mult, ["yr", ("ggb", bs)], ["yrb"])
                yield
                b3 = bank(3).bitcast(BF16)
                O.trs([(b3[:, q * 128:(q + 1) * 128], yrb[:, q * 128:(q + 1) * 128]) for q in range(4)], identb[:], ["yrb", "identb"], [PK(3)])
                yield
                O.copy("act", catT[cs_][:, 0:4, :], q4v(b3[:, 0:512]), [], [PK(3), ("catT", cs_, 0)])
                yield

            def chain_A(G, t):
                sq, g = divmod(G, NG)
                bs = G % 2
                n = g * 4 + t
                gt = G * 4 + t
                cs_ = gt % 2
                c0, c1 = t * 128, (t + 1) * 128
                kbs = [kb_ for kb_ in range(3) if 0 <= n + kb_ - 1 < NT]
                si = 0
                for gk in range(2):
                    for kb_ in kbs:
                        sb_ = si % 2
                        si += 1
                        kc0 = (t + kb_) * 128
                        items = [(bank(sb_), identb[:], attb[:, 0, kb_, gk, :], True, True)]
                        for pp in range(2):
                            items.append((bank(sb_)[:, pp * 256:(pp + 1) * 256], akp[bs][gk][pp][:, kc0:kc0 + 128],
                                          aqb[bs][:, 2 * gk:2 * gk + 2, c0:c1], False, True, True))
                        O.mms(items, AKPK(bs) + [("aqb", bs), "attb", "identb"], [PK(sb_)])
                        yield
                        yield
                        pview = Pb[:, kb_, 4 * gk:4 * gk + 4, :].rearrange("p (bi pp) q -> p pp bi q", pp=2)
                        sview = bank(sb_).rearrange("p (pp bi q) -> p pp bi q", pp=2, bi=2)
                        O.act(pview, sview, AF.Exp, [], [PK(sb_), ("Pb", gk)])
                        yield
                    items = []
                    for hh in range(4):
                        h = 4 * gk + hh
                        for i_, kb_ in enumerate(kbs):
                            items.append((bank(4 + gk)[:, hh * 65:(hh + 1) * 65], Pb[:, kb_, h, :], avb[bs][:, t + kb_, gk * 65:(gk + 1) * 65],
                                          i_ == 0, i_ == len(kbs) - 1))
                    O.mms(items, [("Pb", gk), ("avb", bs)], [PK(4 + gk)])
                    yield
                    pv3 = bank(4 + gk)[:, 0:260].rearrange("p (h c) -> p h c", h=4)
                    O.tt(den[:, gk * 4:(gk + 1) * 4].unsqueeze(2), pv3[:, :, 64:65], esk[:, gk * 4:(gk + 1) * 4].unsqueeze(2), ALU.add,
                         ["esk"], [PK(4 + gk), ("den", gk)])
                    yield
                    O.recip(den[:, 8 + gk * 4:8 + (gk + 1) * 4], den[:, gk * 4:(gk + 1) * 4], [("den", gk)], [("rden", gk)])
                    yield
                    O.tt(yab[:, gk * 256:(gk + 1) * 256].rearrange("p (h c) -> p h c", h=4), pv3[:, :, 0:64],
                         den[:, 8 + gk * 4:8 + (gk + 1) * 4].unsqueeze(2).to_broadcast([128, 4, 64]), ALU.mult,
                         [("rden", gk)], [PK(4 + gk), ("ya", gk)])
                    yield
                b0 = bank(0).bitcast(BF16)
                O.trs([(b0[:, q * 128:(q + 1) * 128], yab[:, q * 128:(q + 1) * 128]) for q in range(4)], identb[:],
                      [("ya", 0), ("ya", 1), "identb"], [PK(0)])
                yield
                O.copy("act", catT[cs_][:, 4:8, :], q4v(b0[:, 0:512]), [], [PK(0), ("catT", cs_, 1)])
                yield

            def chain_O(G, t):
                gt = G * 4 + t
                cs_ = gt % 2
                xsl = gt % 2
                hsl = gt % 2
                hs = G % 2
                c0, c1 = t * 128, (t + 1) * 128
                for nb in range(2):
                    O.acc(bank(6 + nb), [(catT[cs_][:, kc, :], wo[:, kc, nb * 512:(nb + 1) * 512]) for kc in range(8)],
                          [("catT", cs_, 0), ("catT", cs_, 1)] + WOK, [PK(6 + nb)])
                    yield
                    yield
                    yield
                    O.stt(r_t[:, nb * 512:(nb + 1) * 512], xsb[xsl][:, nb * 512:(nb + 1) * 512], ALPHA, bank(6 + nb), ALU.mult, ALU.add,
                          [("xsB", xsl)], [PK(6 + nb), ("r_t", nb)])
                    yield
                rk = [("r_t", 0), ("r_t", 1)]
                O.bnstats([(st6[:, 0:6], r_t[:, 0:512]), (st6[:, 6:12], r_t[:, 512:1024])], rk, ["Bst"])
                yield
                O.bnaggr(mv[:, 0:2], st6[:, 0:12], ["Bst"], ["Bmv"])
                yield
                O.ts(mv[:, 2:3], mv[:, 1:2], LN_EPS, None, ALU.add, None, ["Bmv"], ["Bve"])
                yield
                O.act(mv[:, 5:6], mv[:, 2:3], AF.Ln, ["Bve"], ["Blnve"])
                yield
                O.act(mv[:, 3:4], mv[:, 5:6], AF.Exp, ["Blnve"], ["Brs"], scale=-0.5)
                yield
                O.stt(mv[:, 4:5], mv[:, 0:1], -1.0, mv[:, 3:4], ALU.mult, ALU.mult, ["Bmv", "Brs"], ["Bnm"])
                yield
                hk = ("h_t", hsl)
                O.act(h_t[hsl][:], r_t[:], AF.Identity, rk + ["Brs", "Bnm"], [hk], scale=mv[:, 3:4], bias=mv[:, 4:5])
                yield
                for hb in range(2):
                    O.trs([(bank(6 + hb)[:, q * 128:(q + 1) * 128], h_t[hsl][:, (hb * 4 + q) * 128:(hb * 4 + q + 1) * 128]) for q in range(4)],
                          ident, [hk, "tabs"], [PK(6 + hb)])
                    yield
                for hb in range(2):
                    O.acts_sb([(hT_st[hs][:, hb * 4 + q, c0:c1], bank(6 + hb)[:, q * 128:(q + 1) * 128], AF.Identity,
                                gbcol[:, hb * 4 + q:hb * 4 + q + 1], gbcol[:, 8 + hb * 4 + q:8 + hb * 4 + q + 1]) for q in range(4)],
                              [("gbcol", 0), ("gbcol", 1)], [PK(6 + hb), ("hTst", hs)])
                    yield
                O.tt(h_t[hsl][:], h_t[hsl][:], lnp[:, 0, :], ALU.mult, LNPK, [hk])
                yield
                O.tt(h_t[hsl][:], h_t[hsl][:], lnp[:, 1, :], ALU.add, [hk] + LNPK, [hk])
                yield
                O.dma("pool", H32[gt], h_t[hsl][:], [hk], [], "sth%d" % hsl)
                if t == 3:
                    O.dma("pool", HT[G], fl(hT_st[hs][:]), [("hTst", hs)], [], "stht%d" % hs)

            def run_interleaved(gens):
                gens = list(gens)
                while gens:
                    for g_ in list(gens):
                        try:
                            next(g_)
                        except StopIteration:
                            gens.remove(g_)

            load_group_B(0)
            O.dma("sp", gbcol[:, 0:8], ln1g[0].rearrange("(kc p) -> p kc", p=128), [], [("gbcol", 0)], "cb4", slow=True)
            O.dma("sp", gbcol[:, 8:16], ln1b[0].rearrange("(kc p) -> p kc", p=128), [], [("gbcol", 1)], "cb5", slow=True)
            prev = None
            for G in range(NGT):
                if G + 1 < NGT:
                    load_group_B(G + 1)
                for t in range(4):
                    gens = [chain_R(G, t), chain_A(G, t)]
                    if prev is not None:
                        gens.append(chain_O(*prev))
                    run_interleaved(gens)
                    prev = (G, t)
            run_interleaved([chain_O(*prev)])

        P.barrier()

        pcd = top.enter_context(contextlib.ExitStack())
        wd = sbuf(pcd, "wd", [128, NJB, D], BF16)
        wpg = sbuf(pcd, "wpg", [128, 8, D], BF16)
        with contextlib.ExitStack() as pc:
            wg = sbuf(pc, "wg", [128, 8, FF], BF16)
            wu = sbuf(pc, "wu", [128, 8, FF], BF16)
            hTc = [sbuf(pc, "hTc%d" % i, [128, 8, 512], BF16) for i in range(2)]
            act_st = [sbuf(pc, "actst%d" % i, [128, NJB // 2, 512], BF16) for i in range(2)]
            sgc = [sbuf(pc, "sgc%d" % i, [128, 512], F32) for i in range(3)]
            w_g_v = w_g.rearrange("(kc p) f -> p kc f", p=128)
            w_u_v = w_u.rearrange("(kc p) f -> p kc f", p=128)
            NCH = (NJB + 3) // 4
            for c in range(NCH):
                cs = slice(c * 512, min(FF, (c + 1) * 512))
                O.dma("pool", wg[:, :, cs], w_g_v[:, :, cs], [], [("wg", c)], "wg%d" % c)
                O.dma("pool", wu[:, :, cs], w_u_v[:, :, cs], [], [("wu", c)], "wu%d" % c)
            w_d_v = w_d.rearrange("(jb p) f -> p jb f", p=128)
            for j0 in range(0, NJB, 2):
                O.dma("pool", wd[:, j0:j0 + 2, :], w_d_v[:, j0:j0 + 2, :], [], [("wd", j0)], "wd")
            w_pg_v = w_pg.rearrange("(kc p) f -> p kc f", p=128)
            for kc in range(0, 8, 4):
                O.dma("pool", wpg[:, kc:kc + 4, :], w_pg_v[:, kc:kc + 4, :], [], [("wpg", kc)], "wpg")

            def load_C1(G):
                sl = G % 2
                O.dma("sp", fl(hTc[sl][:]), HT[G], [], [("hTc", sl)], "lht%d" % sl)

            cc = {"pi": 0}

            def group_C1(G):
                sl = G % 2
                hlf = NJB // 2
                for jb in range(NJB):
                    pi = cc["pi"]
                    cc["pi"] += 1
                    ba, bb = (pi % 4) * 2, (pi % 4) * 2 + 1
                    ssl = pi % 3
                    O.acc(bank(ba), [(wg[:, kc, jb * 128:(jb + 1) * 128], hTc[sl][:, kc, :]) for kc in range(8)], [("wg", jb // 4), ("hTc", sl)], [PK(ba)])
                    O.acc(bank(bb), [(wu[:, kc, jb * 128:(jb + 1) * 128], hTc[sl][:, kc, :]) for kc in range(8)], [("wu", jb // 4), ("hTc", sl)], [PK(bb)])
                    O.act(sgc[ssl][:], bank(ba), AF.Silu, [], [PK(ba), ("sgc", ssl)])
                    hh_ = jb // hlf
                    O.tt(act_st[hh_][:, jb - hh_ * hlf, :], sgc[ssl][:], bank(bb), ALU.mult, [("sgc", ssl)], [PK(bb), ("actst", hh_)])
                    if jb == hlf - 1:
                        O.dma("pool", ACTS[G][:, 0:hlf * 512], fl(act_st[0][:]), [("actst", 0)], [], "sta0")
                O.dma("pool", ACTS[G][:, hlf * 512:NJB * 512], fl(act_st[1][:]), [("actst", 1)], [], "sta1")

            load_C1(0)
            for G in range(NGT):
                if G + 1 < NGT:
                    load_C1(G + 1)
                group_C1(G)

        P.barrier()

        with contextlib.ExitStack() as pd_:
            wpe = sbuf(pd_, "wpe", [128, 2, D], BF16)
            lnp2 = sbuf(pd_, "lnp2", [128, 2, D], F32)
            actc = [sbuf(pd_, "actc%d" % i, [128, NJB, 512], BF16) for i in range(2)]
            hTd = [sbuf(pd_, "hTd%d" % i, [128, 8, 512], BF16) for i in range(2)]
            h32 = [sbuf(pd_, "h32%d" % i, [128, D], F32) for i in range(2)]
            pt = [sbuf(pd_, "pt%d" % i, [128, PLE], F32) for i in range(2)]
            pT = [sbuf(pd_, "pT%d" % i, [128, 2, 128], BF16) for i in range(2)]
            sig = sbuf(pd_, "sig", [128, D], F32)
            z_t = [sbuf(pd_, "z_t%d" % i, [128, D], F32) for i in range(2)]
            nhalf2 = sbuf(pd_, "nhalf2", [128, 1], F32)
            o_t = [sbuf(pd_, "o_t%d" % i, [128, D], F32) for i in range(2)]
            st6 = sbuf(pd_, "st6D", [128, 12], F32)
            mv = sbuf(pd_, "mvD", [128, 8], F32)
            sdv = sbuf(pd_, "sdD", [128, 1], F32)
            rsv = sbuf(pd_, "rsD", [128, 1], F32)

            WDK = []
            WPGK = []
            O.dma("pool", wpe[:], w_pe.rearrange("(c p) f -> p c f", p=128), [], ["wpe"], "wpe")
            O.dma("sp", lnp2[:, 0, :], ln2g[0].partition_broadcast(128), [], [("lnpx", 0)], "cd0")
            O.dma("sp", lnp2[:, 1, :], ln2b[0].partition_broadcast(128), [], [("lnpx", 1)], "cd1")
            LNPK2 = [("lnpx", 0), ("lnpx", 1)]
            O.memsets([(nhalf2[:], -0.5)], ["nhalf2"])
            hlf = NJB // 2

            def load_C2(G):
                sl = G % 2
                O.dma("sp", fl(actc[sl][:, 0:hlf, :]), ACTS[G][:, 0:hlf * 512], [], [("actc", sl, 0)], "lac%d0" % sl)
                O.dma("sp", fl(actc[sl][:, hlf:NJB, :]), ACTS[G][:, hlf * 512:NJB * 512], [], [("actc", sl, 1)], "lac%d1" % sl)
                O.dma("sp", fl(hTd[sl][:]), HT[G], [], [("hTd", sl)], "lhd%d" % sl)

            def load_tile_C2(gt):
                sl = gt % 2
                O.dma("sp", h32[sl][:], H32[gt], [], [("h32", sl)], "lh32%d" % sl)
                O.dma("sp", pt[sl][:], pin[gt * 128:(gt + 1) * 128, :], [], [("pt", sl)], "lpt%d" % sl)

            def chain_M(G, t):
                sl = G % 2
                gt = G * 4 + t
                tsl = gt % 2
                zs = gt % 2
                c0, c1 = t * 128, (t + 1) * 128
                O.trs([(bank(6)[:, c * 128:(c + 1) * 128], pt[tsl][:, c * 128:(c + 1) * 128]) for c in range(2)], ident,
                      [("pt", tsl), "tabs"], [PK(6)])
                yield
                O.copy("act", pT[tsl][:], bank(6)[:, 0:256].rearrange("p (a b) -> p a b", a=2), [], [PK(6), ("pT", tsl)])
                yield
                for nb in range(2):
                    cs = slice(nb * 512, (nb + 1) * 512)
                    O.acc(bank(2 + nb), [(hTd[sl][:, kc, c0:c1], wpg[:, kc, cs]) for kc in range(8)], [("hTd", sl)] + WPGK, [PK(2 + nb)])
                    yield
                    O.acc(bank(4 + nb), [(pT[tsl][:, c, :], wpe[:, c, cs]) for c in range(2)], [("pT", tsl), "wpe"], [PK(4 + nb)])
                    yield
                    O.acc(bank(nb), [(actc[sl][:, jb, c0:c1], wd[:, jb, cs]) for jb in range(NJB)],
                          [("actc", sl, 0), ("actc", sl, 1)] + WDK, [PK(nb)])
                    yield
                for nb in range(2):
                    cs = slice(nb * 512, (nb + 1) * 512)
                    O.act(sig[:, cs], bank(2 + nb), AF.Sigmoid, [], [PK(2 + nb), ("sig", nb)])
                    yield
                    O.tt(sig[:, cs], sig[:, cs], bank(4 + nb), ALU.mult, [("sig", nb)], [PK(4 + nb), ("sig", nb)])
                    yield
                    O.stt(z_t[zs][:, cs], h32[tsl][:, cs], ALPHA, bank(nb), ALU.mult, ALU.add, [("h32", tsl)], [PK(nb), ("z_t", zs, nb)])
                    yield
                    O.tt(z_t[zs][:, cs], z_t[zs][:, cs], sig[:, cs], ALU.add, [("z_t", zs, nb), ("sig", nb)], [("z_t", zs, nb)])
                    yield

            def chain_L(G, t):
                gt = G * 4 + t
                tsl = gt % 2
                zs = gt % 2
                zk = [("z_t", zs, 0), ("z_t", zs, 1)]
                z = z_t[zs]
                O.bnstats([(st6[:, 0:6], z[:, 0:512]), (st6[:, 6:12], z[:, 512:1024])], zk, ["Dst"])
                yield
                O.bnaggr(mv[:, 0:2], st6[:, 0:12], ["Dst"], ["Dmv"])
                yield
                O.ts(mv[:, 2:3], mv[:, 1:2], LN_EPS, None, ALU.add, None, ["Dmv"], ["Dve"])
                yield
                O.tt(mv[:, 3:4], mv[:, 2:3], nhalf2[:, 0:1], ALU.pow, ["Dve", "nhalf2"], ["Drs"], eng="pool")
                yield
                O.stt(mv[:, 4:5], mv[:, 0:1], -1.0, mv[:, 3:4], ALU.mult, ALU.mult, ["Dmv", "Drs"], ["Dnm"])
                yield
                ok = ("o_t", tsl)
                O.act(o_t[tsl][:], z[:], AF.Identity, zk + ["Drs", "Dnm"], [ok], scale=mv[:, 3:4], bias=mv[:, 4:5])
                yield
                O.tt(o_t[tsl][:], o_t[tsl][:], lnp2[:, 0, :], ALU.mult, [ok] + LNPK2, [ok])
                yield
                O.tt(o_t[tsl][:], o_t[tsl][:], lnp2[:, 1, :], ALU.add, [ok] + LNPK2, [ok])
                yield
                fo = O.dma("pool", out[gt * 128:(gt + 1) * 128, :], o_t[tsl][:], [ok], [], "sto%d" % tsl)
                final_ops.append(fo)

            def run_il(gens):
                gens = list(gens)
                while gens:
                    for g_ in list(gens):
                        try:
                            next(g_)
                        except StopIteration:
                            gens.remove(g_)

            load_C2(0)
            load_tile_C2(0)
            prev = None
            for G in range(NGT):
                if G + 1 < NGT:
                    load_C2(G + 1)
                for t in range(4):
                    gt = G * 4 + t
                    if gt + 1 < NTT:
                        load_tile_C2(gt + 1)
                    gens = [chain_M(G, t)]
                    if prev is not None:
                        gens.append(chain_L(*prev))
                    run_il(gens)
                    prev = (G, t)
            run_il([chain_L(*prev)])

        P.emit(final_ops=final_ops[-2:])
    return nc


_NC_CACHE = {}


def kernel(x, p, w_in, ret_decay_fwd, ret_decay_bwd, ret_gn_gain, attn_sink, w_out,
           ln1_gain, ln1_bias, w_ffn_gate, w_ffn_up, w_ffn_down, w_ple_proj, w_ple_gate,
           ln2_gain, ln2_bias, _debug=False):
    x = np.asarray(x)
    B, S, _ = x.shape
    ncores = min(8, B)
    NSEQ = B // ncores
    key = (NSEQ, S, _debug)
    if key not in _NC_CACHE:
        _NC_CACHE[key] = build(NSEQ, S, debug=_debug)
    nc = _NC_CACHE[key]
    tabs, colv, attb = _const_tables()
    f = lambda a: np.ascontiguousarray(np.asarray(a, dtype=np.float32))
    shared = {
        "w_in": f(w_in)[0], "ret_decay_fwd": f(ret_decay_fwd), "ret_decay_bwd": f(ret_decay_bwd),
        "ret_gn_gain": f(ret_gn_gain), "attn_sink": f(attn_sink), "w_out": f(w_out)[0],
        "ln1_gain": f(ln1_gain), "ln1_bias": f(ln1_bias), "w_ffn_gate": f(w_ffn_gate)[0],
        "w_ffn_up": f(w_ffn_up)[0], "w_ffn_down": f(w_ffn_down)[0], "w_ple_proj": f(w_ple_proj)[0],
        "w_ple_gate": f(w_ple_gate)[0], "ln2_gain": f(ln2_gain), "ln2_bias": f(ln2_bias),
        "c_tabs": tabs, "c_colv": colv, "c_attb": attb,
    }
    xf = f(x).reshape(ncores, NSEQ * S, D)
    pf = f(p)[0].reshape(ncores, NSEQ * S, PLE)
    in_maps = []
    for c in range(ncores):
        m = dict(shared)
        m["x"] = xf[c]
        m["p"] = pf[c]
        in_maps.append(m)
    res = run_bass_kernel_spmd(nc, in_maps, core_ids=list(range(ncores)))
    if _debug:
        return res
    outp = np.stack([np.asarray(r["out"]) for r in res.results], axis=0)
    return outp.reshape(B, S, D).astype(np.float32)
```

```python
import contextlib
import math
import numpy as np
import concourse.bass as bass
import concourse.mybir as mybir
from concourse.bass_utils import run_bass_kernel_spmd

F32 = mybir.dt.float32
BF16 = mybir.dt.bfloat16
AF = mybir.ActivationFunctionType
ALU = mybir.AluOpType
AX = mybir.AxisListType

D = 1024
FF = 2816
NJB = FF // 128
INW = 2816
PLE = 256
ALPHA = 2.0 ** 0.25
LN_EPS = 1e-5
GN_EPS = 1e-5
LN2C = math.log(2.0)
LN8 = math.log(0.125)
BIG = 1.0e9

ENGS = ("pe", "act", "dve", "pool", "sp")


class _Op:
    __slots__ = ("eng", "fn", "deps", "dma_sem", "token", "signal", "idx", "is_dma")


class Prog:
    def __init__(self, nc):
        self.nc = nc
        self.ops = []
        self.last_w = {}
        self.readers = {}
        self.phys = {}
        self.phys_count = {}
        self.free_phys_q = {}
        self.phys_q = {}
        self.pending_barrier = {}

    def op(self, eng, fn, reads=(), writes=(), dma=None):
        o = _Op()
        o.eng = eng
        o.fn = fn
        o.is_dma = dma is not None
        o.dma_sem = dma
        o.signal = o.is_dma
        o.idx = len(self.ops)
        deps = []
        for k in reads:
            w = self.last_w.get(k)
            if w is not None:
                deps.append((w, "raw"))
        for k in writes:
            w = self.last_w.get(k)
            if w is not None:
                deps.append((w, "waw"))
            for r in self.readers.get(k, ()):
                deps.append((r, "war"))
        bar = self.pending_barrier.pop(eng, None)
        if bar is not None:
            for d in bar:
                deps.append((d, "bar"))
        od = []
        seen = set()
        for (d, kind) in deps:
            if d is o or d.idx in seen:
                continue
            if d.eng == eng and not d.is_dma and not o.is_dma:
                if kind != "raw" or eng == "pe":
                    continue
            seen.add(d.idx)
            od.append(d)
            d.signal = True
        o.deps = od
        for k in reads:
            self.readers.setdefault(k, []).append(o)
        for k in writes:
            self.last_w[k] = o
            self.readers[k] = []
        if o.is_dma:
            pid = self.phys.get(dma)
            if pid is None:
                fp = self.free_phys_q.setdefault(eng, [])
                pid = fp.pop() if fp else len(self.phys_count)
                self.phys[dma] = pid
                self.phys_q[pid] = eng
                self.phys_count.setdefault(pid, 0)
            c = self.phys_count[pid] + 16
            self.phys_count[pid] = c
            o.dma_sem = pid
            o.token = (("dma", pid), c)
        self.ops.append(o)
        return o

    def barrier(self):
        tails = {}
        for o in self.ops:
            if o.is_dma:
                tails[("dma", o.dma_sem)] = o
            else:
                tails[("eng", o.eng)] = o
        tl = list(tails.values())
        for e in ENGS:
            self.pending_barrier[e] = list(tl)
        self.last_w = {}
        self.readers = {}
        for pid in sorted(self.phys.values(), reverse=True):
            self.free_phys_q.setdefault(self.phys_q[pid], []).append(pid)
        self.phys = {}

    def emit(self, final_ops=()):
        nc = self.nc
        cnt = {e: 0 for e in ENGS}
        for o in self.ops:
            if not o.is_dma and o.signal:
                cnt[o.eng] += 1
                o.token = (("eng", o.eng), cnt[o.eng])
        sem_names = [("eng", e) for e in ENGS] + [("dma", k) for k in self.phys_count]
        with contextlib.ExitStack() as st:
            sems = {}
            for i, sn in enumerate(sem_names):
                sems[sn] = st.enter_context(nc.semaphore("s%d" % i))
            block = st.enter_context(nc.Block())
            per_eng = {e: [o for o in self.ops if o.eng == e] for e in ENGS}

            def body(e):
                def run(engh):
                    waited = {}

                    def wait(tok):
                        sn, v = tok
                        if waited.get(sn, 0) >= v:
                            return
                        waited[sn] = v
                        engh.wait_ge(sems[sn], v)

                    for o in per_eng[e]:
                        need = {}
                        for d in o.deps:
                            sn, v = d.token
                            if need.get(sn, 0) < v:
                                need[sn] = v
                        for sn, v in need.items():
                            wait((sn, v))
                        inst = o.fn(engh)
                        if o.is_dma:
                            inst.then_inc(sems[("dma", o.dma_sem)], 16)
                        elif o.signal:
                            inst.then_inc(sems[o.token[0]], 1)
                    if e == "sp":
                        for fo in final_ops:
                            wait(fo.token)
                return run

            block.tensor(body("pe"))
            block.scalar(body("act"))
            block.vector(body("dve"))
            block.gpsimd(body("pool"))
            block.sync(body("sp"))


def _const_tables():
    i = np.arange(128, dtype=np.float32)
    s = i[:, None]
    c = i[None, :]
    tf = np.where(c >= s, c - s, BIG).astype(np.float32)
    tb = np.where(s >= c, s - c, BIG).astype(np.float32)
    cp1 = np.broadcast_to(c + 1.0, (128, 128)).astype(np.float32)
    c128m = np.broadcast_to(128.0 - c, (128, 128)).astype(np.float32)
    bdt = np.where((np.arange(128)[:, None] // 64) == (np.arange(128)[None, :] // 64), 128.0, BIG).astype(np.float32)
    ident = np.eye(128, dtype=np.float32)
    tabs = np.stack([tf, tb, cp1, c128m, bdt, ident], axis=1)
    colv = np.stack([127.0 - i, i], axis=1).astype(np.float32)
    slopes = 2.0 ** (-(np.arange(8, dtype=np.float64) + 1.0))
    j = np.arange(128)[:, None]
    q = np.arange(128)[None, :]
    attb = np.zeros((3, 128, 8, 128), dtype=np.float32)
    for h in range(8):
        dl = 128 + q - j
        attb[0, :, h, :] = np.where(j >= q, -slopes[h] * dl, -1.0e30)
        attb[1, :, h, :] = -slopes[h] * np.abs(q - j)
        dr = 128 + j - q
        attb[2, :, h, :] = np.where(j <= q, -slopes[h] * dr, -1.0e30)
    import ml_dtypes
    a = attb.transpose(1, 0, 2, 3).reshape(128, 3, 2, 2, 2, 128)
    a = a.transpose(0, 1, 2, 4, 3, 5).reshape(128, 3, 2, 512)
    hi = a.astype(ml_dtypes.bfloat16).astype(np.float32)
    lo = (a - hi).astype(ml_dtypes.bfloat16).astype(np.float32)
    attb2 = np.ascontiguousarray(np.stack([hi, lo], axis=1))
    return np.ascontiguousarray(tabs), colv, attb2


class Ops:
    def __init__(self, P):
        self.P = P

    def mms(self, items, reads, writes):
        items = [tuple(it) for it in items]

        def fn(e):
            inst = None
            for it in items:
                o, l, r, s0, s1 = it[:5]
                if len(it) > 5 and it[5]:
                    inst = e.matmul(o, lhsT=l, rhs=r, start=s0, stop=s1, skip_group_check=True)
                else:
                    inst = e.matmul(o, lhsT=l, rhs=r, start=s0, stop=s1)
            return inst
        return self.P.op("pe", fn, reads, writes)

    def acc(self, out, pairs, reads, writes):
        pairs = list(pairs)
        n = len(pairs)
        return self.mms([(out, l, r, i == 0, i == n - 1) for i, (l, r) in enumerate(pairs)], reads, writes)

    def trs(self, items, ident, reads, writes):
        items = list(items)

        def fn(e):
            inst = None
            for (o, i_) in items:
                inst = e.transpose(o, i_, ident)
            return inst
        return self.P.op("pe", fn, reads, writes)

    def act(self, out, in_, func, reads, writes, scale=None, bias=None):
        kw = {}
        if scale is not None:
            kw["scale"] = scale
        if bias is not None:
            kw["bias"] = bias
        return self.P.op("act", lambda e: e.activation(out=out, in_=in_, func=func, **kw), reads, writes)

    def acts(self, items, reads, writes):
        items = list(items)

        def fn(e):
            inst = None
            for (o, i_, f_) in items:
                inst = e.activation(out=o, in_=i_, func=f_)
            return inst
        return self.P.op("act", fn, reads, writes)

    def tt(self, out, in0, in1, alu, reads, writes, eng="dve"):
        return self.P.op(eng, lambda e: e.tensor_tensor(out=out, in0=in0, in1=in1, op=alu), reads, writes)

    def stt(self, out, in0, scalar, in1, op0, op1, reads, writes):
        return self.P.op("dve", lambda e: e.scalar_tensor_tensor(out=out, in0=in0, scalar=scalar, in1=in1, op0=op0, op1=op1),
                         reads, writes)

    def stts(self, items, reads, writes):
        items = list(items)

        def fn(e):
            inst = None
            for (o, i0, sc, i1, op0, op1) in items:
                inst = e.scalar_tensor_tensor(out=o, in0=i0, scalar=sc, in1=i1, op0=op0, op1=op1)
            return inst
        return self.P.op("dve", fn, reads, writes)

    def ts(self, out, in0, s1, s2, op0, op1, reads, writes):
        if s2 is None:
            return self.P.op("dve", lambda e: e.tensor_scalar(out=out, in0=in0, scalar1=s1, scalar2=None, op0=op0), reads, writes)
        return self.P.op("dve", lambda e: e.tensor_scalar(out=out, in0=in0, scalar1=s1, scalar2=s2, op0=op0, op1=op1), reads, writes)

    def copy(self, eng, out, in_, reads, writes, scale=None):
        if eng == "act":
            if scale is not None:
                return self.P.op("act", lambda e: e.activation(out=out, in_=in_, func=AF.Copy, scale=scale), reads, writes)
            return self.P.op("act", lambda e: e.activation(out=out, in_=in_, func=AF.Copy), reads, writes)
        assert scale is None
        return self.P.op(eng, lambda e: e.tensor_copy(out=out, in_=in_), reads, writes)

    def memsets(self, items, writes, eng="dve"):
        items = list(items)

        def fn(e):
            inst = None
            for (ap, v) in items:
                inst = e.memset(ap, v)
            return inst
        return self.P.op(eng, fn, (), writes)

    def reduce(self, out, in_, alu, reads, writes):
        return self.P.op("dve", lambda e: e.tensor_reduce(out=out, in_=in_, axis=AX.X, op=alu), reads, writes)

    def recip(self, out, in_, reads, writes):
        return self.P.op("dve", lambda e: e.reciprocal(out=out, in_=in_), reads, writes)

    def bnstats(self, items, reads, writes):
        items = list(items)

        def fn(e):
            inst = None
            for (o, i_) in items:
                inst = e.bn_stats(out=o, in_=i_)
            return inst
        return self.P.op("dve", fn, reads, writes)

    def bnaggr(self, out, in_, reads, writes):
        return self.P.op("dve", lambda e: e.bn_aggr(out=out, in_=in_), reads, writes)

    def dma(self, eng, out, in_, reads, writes, sem, slow=False):
        if slow:
            return self.P.op(eng, lambda e: e.dma_start(out=out, in_=in_, allow_slow_non_contiguous=True), reads, writes, dma=sem)
        return self.P.op(eng, lambda e: e.dma_start(out=out, in_=in_), reads, writes, dma=sem)

    def acts_sb(self, items, reads, writes):
        items = list(items)

        def fn(e):
            inst = None
            for (o, i_, f_, sc_, bi_) in items:
                inst = e.activation(out=o, in_=i_, func=f_, scale=sc_, bias=bi_)
            return inst
        return self.P.op("act", fn, reads, writes)


def build(NSEQ, S, debug=False):
    NT = S // 128
    NG = S // 512
    TOK = NSEQ * S
    NGT = NSEQ * NG
    NTT = NSEQ * NT
    nc = bass.Bass("TRN2", target_bir_lowering=False)

    def din(name, shape, dt=F32):
        return nc.dram_tensor(name, list(shape), dt, kind="ExternalInput").ap()

    x = din("x", [TOK, D])
    pin = din("p", [TOK, PLE])
    w_in = din("w_in", [D, INW])
    dfw = din("ret_decay_fwd", [1, 8])
    dbw = din("ret_decay_bwd", [1, 8])
    gng = din("ret_gn_gain", [1, 512])
    sink = din("attn_sink", [1, 8])
    w_out = din("w_out", [D, D])
    ln1g = din("ln1_gain", [1, D])
    ln1b = din("ln1_bias", [1, D])
    w_g = din("w_ffn_gate", [D, FF])
    w_u = din("w_ffn_up", [D, FF])
    w_d = din("w_ffn_down", [FF, D])
    w_pe = din("w_ple_proj", [PLE, D])
    w_pg = din("w_ple_gate", [D, D])
    ln2g = din("ln2_gain", [1, D])
    ln2b = din("ln2_bias", [1, D])
    tabs_d = din("c_tabs", [128, 6, 128])
    colv_d = din("c_colv", [128, 2])
    attb_d = din("c_attb", [128, 2, 3, 2, 512])
    out = nc.dram_tensor("out", [TOK, D], F32, kind="ExternalOutput").ap()

    skind = "ExternalOutput" if debug else "Internal"

    def dscr(name, shape, dt):
        return nc.dram_tensor(name, list(shape), dt, kind=skind).ap()

    QT = dscr("s_qt", [NGT, 128, 4 * 512], BF16)
    KT = dscr("s_kt", [NGT, 128, 4 * 512], BF16)
    AQT = dscr("s_aqt", [NGT, 128, 4 * 512], BF16)
    VV = dscr("s_v", [NGT, 128, 4 * 512], BF16)
    GGS = dscr("s_gg", [NGT, 128, 4 * 512], BF16)
    AKT = dscr("s_akt", [NSEQ, 128, S], BF16)
    AV1 = dscr("s_av1", [NSEQ, NT, 128, 130], BF16)
    RF = dscr("s_rf", [NSEQ, 128, NT * 512], BF16)
    RB = dscr("s_rb", [NSEQ, 128, NT * 512], BF16)
    H32 = dscr("s_h32", [NTT, 128, D], F32)
    HT = dscr("s_ht", [NGT, 128, 8 * 512], BF16)
    ACTS = dscr("s_act", [NGT, 128, NJB * 512], BF16)

    P = Prog(nc)
    O = Ops(P)
    final_ops = []

    def fl(ap3):
        return ap3.rearrange("p a b -> p (a b)")

    def h8(ap2):
        return ap2.rearrange("p (h d) -> p h d", h=8)

    def q4v(ap2):
        return ap2.rearrange("p (a b) -> p a b", a=4)

    with contextlib.ExitStack() as top:
        def sbuf(stack, name, shape, dt):
            return stack.enter_context(nc.sbuf_tensor(name, list(shape), dt))

        ps = top.enter_context(nc.psum_tensor("ps", [128, 8 * 512], F32))

        def bank(i):
            return ps[:, i * 512:(i + 1) * 512]

        def PK(i):
            return ("ps", i)

        tabs = sbuf(top, "tabs", [128, 6, 128], F32)
        colv = sbuf(top, "colv", [128, 2], F32)
        dd = sbuf(top, "dd", [128, 24], F32)
        de = sbuf(top, "de", [128, 24], F32)
        lg = sbuf(top, "lg", [128, 24], F32)
        cst = sbuf(top, "cst", [128, 4], F32)
        ident = tabs[:, 5, :]

        O.dma("sp", tabs[:], tabs_d, [], ["tabs"], "c0")
        O.dma("sp", colv[:], colv_d, [], ["colv"], "c1")
        O.dma("sp", dd[:, 0:8], dfw[0].partition_broadcast(128), [], [("dd", 0)], "c2")
        O.dma("sp", dd[:, 8:16], dbw[0].partition_broadcast(128), [], [("dd", 1)], "c3")
        ddk = [("dd", 0), ("dd", 1), ("dd", 2)]
        for t in range(2):
            O.copy("dve", dd[t * 64:(t + 1) * 64, 16:20], dd[t * 64:(t + 1) * 64, t:8:2], [("dd", 0)], [("dd", 2)])
            O.copy("dve", dd[t * 64:(t + 1) * 64, 20:24], dd[t * 64:(t + 1) * 64, 8 + t:16:2], [("dd", 1)], [("dd", 2)])
        O.memsets([(cst[:, 0:1], LN_EPS), (cst[:, 1:2], GN_EPS), (cst[:, 2:3], LN8), (cst[:, 3:4], 1.0)], ["cst"])
        O.act(de[:], dd[:], AF.Exp, ddk, ["de"], scale=LN2C)
        O.act(lg[:], de[:], AF.Ln, ["de", "cst"], ["lg"], scale=-1.0, bias=cst[:, 3:4])

        maskT = sbuf(top, "maskT", [128, 2, 4, 128], F32)
        Gf = sbuf(top, "Gf", [128, 4, 128], F32)
        Gb = sbuf(top, "Gb", [128, 4, 128], F32)
        mtmp = [sbuf(top, "mtmp%d" % i, [128, 128], F32) for i in range(2)]
        cdcol = sbuf(top, "cdcol", [128, 8], F32)
        for h in range(8):
            j, pp = h // 2, h % 2
            O.act(mtmp[0][:], tabs[:, 0, :], AF.Exp, ["lg", "tabs", "cst"], ["mtmp0"], scale=lg[:, h:h + 1], bias=cst[:, 2:3])
            O.act(mtmp[1][:], tabs[:, 1, :], AF.Exp, ["lg", "tabs", "cst"], ["mtmp1"], scale=lg[:, 8 + h:9 + h], bias=cst[:, 2:3])
            O.tt(maskT[:, pp, j, :], mtmp[0][:], mtmp[1][:], ALU.add, ["mtmp0", "mtmp1"], ["maskT"])
        for j in range(4):
            O.act(Gf[:, j, :], tabs[:, 2, :], AF.Exp, ["lg", "tabs"], ["Gf"], scale=lg[:, 16 + j:17 + j])
            O.act(Gb[:, j, :], tabs[:, 3, :], AF.Exp, ["lg", "tabs"], ["Gb"], scale=lg[:, 20 + j:21 + j])
        O.act(cdcol[:], lg[:, 16:24], AF.Exp, ["lg"], ["cdcol"], scale=128.0)

        def lnorm(z, zkeys, gain, bias, lnkeys, outt, okey, st6, mv, sd, rs, tag):
            k_st, k_mv, k_sd, k_rs = [tag + s_ for s_ in ("st", "mv", "sd", "rs")]
            O.bnstats([(st6[:, 0:6], z[:, 0:512]), (st6[:, 6:12], z[:, 512:1024])], zkeys, [k_st])
            O.bnaggr(mv[:, 0:2], st6[:, 0:12], [k_st], [k_mv])
            O.act(sd[:, 0:1], mv[:, 1:2], AF.Sqrt, [k_mv, "cst"], [k_sd], scale=1.0, bias=cst[:, 0:1])
            O.recip(rs[:, 0:1], sd[:, 0:1], [k_sd], [k_rs])
            O.ts(outt, z, mv[:, 0:1], rs[:, 0:1], ALU.subtract, ALU.mult, zkeys + [k_mv, k_rs], [okey])
            O.tt(outt, outt, gain, ALU.mult, [okey] + lnkeys, [okey])
            O.tt(outt, outt, bias, ALU.add, [okey] + lnkeys, [okey])

        with contextlib.ExitStack() as pa:
            win = sbuf(pa, "win", [128, 8, INW], BF16)
            xs = [sbuf(pa, "xsA%d" % i, [128, D], F32) for i in range(2)]
            xT = [sbuf(pa, "xTA%d" % i, [128, 8, 512], BF16) for i in range(2)]
            qT_st = [sbuf(pa, "qTst%d" % i, [128, 4, 512], BF16) for i in range(2)]
            kT_st = [sbuf(pa, "kTst%d" % i, [128, 4, 512], BF16) for i in range(2)]
            aq_st = [sbuf(pa, "aqst%d" % i, [128, 4, 512], BF16) for i in range(2)]
            ak_st = [sbuf(pa, "akst%d" % i, [128, 512], BF16) for i in range(2)]
            v_st = [sbuf(pa, "vst%d" % i, [128, 4, 512], BF16) for i in range(2)]
            gg_st = [sbuf(pa, "ggst%d" % i, [128, 4, 512], BF16) for i in range(2)]
            av_st = [sbuf(pa, "avst%d" % i, [128, 4, 130], BF16) for i in range(2)]
            kf = [sbuf(pa, "kf%d" % i, [128, 512], BF16) for i in range(2)]
            kb = [sbuf(pa, "kb%d" % i, [128, 512], BF16) for i in range(2)]
            sg = [sbuf(pa, "sgA%d" % i, [128, 512], F32) for i in range(2)]
            kvb = sbuf(pa, "kvb", [128, NT, 512], F32)
            rf_st = sbuf(pa, "rfst", [128, NT, 512], BF16)
            rb_st = sbuf(pa, "rbst", [128, NT, 512], BF16)
            st_f = [sbuf(pa, "stf%d" % i, [128, 512], F32) for i in range(2)]
            st_b = [sbuf(pa, "stb%d" % i, [128, 512], F32) for i in range(2)]
            dk = sbuf(pa, "dk", [128, 16], F32)
            gnr = sbuf(pa, "gnr", [128, 512], F32)

            w_in_v = w_in.rearrange("(kc p) f -> p kc f", p=128)
            WINK = [("win", kc) for kc in range(8)]
            for kc in range(8):
                O.dma("pool", win[:, kc, :], w_in_v[:, kc, :], [], [("win", kc)], "win")
            O.dma("sp", gnr[:], gng[0].partition_broadcast(128), [], ["gnr"], "c6")

            O.act(dk[:, 0:8], lg[:, 0:8], AF.Exp, ["lg", "colv", "cst"], [("dk", 0)], scale=colv[:, 0:1], bias=cst[:, 2:3])
            O.act(dk[:, 8:16], lg[:, 8:16], AF.Exp, ["lg", "colv", "cst"], [("dk", 1)], scale=colv[:, 1:2], bias=cst[:, 2:3])
            O.memsets([(fl(rf_st[:]), 0.0), (fl(rb_st[:]), 0.0), (fl(av_st[0][:]), 1.0), (fl(av_st[1][:]), 1.0)],
                      ["rfst", "rbst", ("avst", 0), ("avst", 1)])

            BT = [0, 1]
            ACC = [2, 3, 4, 5]
            KVF, KVB = 6, 7
            cnt = {"acc": 0, "ev": 0}

            def next_acc():
                b = ACC[cnt["acc"] % 4]
                cnt["acc"] += 1
                return b

            def evac(dst, src, pkey, wkey, scale=None):
                cnt["ev"] += 1
                if scale is not None or cnt["ev"] % 2 == 0:
                    O.copy("act", dst, src, [], [pkey, wkey], scale=scale)
                else:
                    O.copy("dve", dst, src, [], [pkey, wkey])

            def scan_step(st2, cur, rst_n, cdo, kv_src, kvkeys, skey, rkey):
                sv = q4v(st2[cur][:])
                dv = q4v(rst_n)
                O.acts([(dv[0:64, :, 0:64], sv[0:64, :, 0:64], AF.Copy), (dv[64:128, :, 64:128], sv[64:128, :, 64:128], AF.Copy)],
                       [(skey, cur)], [rkey])
                O.stts([(st2[1 - cur][:, j * 128:(j + 1) * 128], st2[cur][:, j * 128:(j + 1) * 128], cdcol[:, cdo + j:cdo + j + 1],
                         kv_src[:, j * 128:(j + 1) * 128], ALU.mult, ALU.add) for j in range(4)],
                       [(skey, cur), "cdcol"], [(skey, 1 - cur)] + kvkeys)

            def chain_X(G):
                gs = G % 2
                for t in range(4):
                    gt = G * 4 + t
                    sl = gt % 2
                    O.dma("sp", xs[sl][:], x[gt * 128:(gt + 1) * 128, :], [], [("xsA", sl)], "xsA%d" % sl)
                    for hb in range(2):
                        O.trs([(bank(BT[hb])[:, q * 128:(q + 1) * 128], xs[sl][:, (hb * 4 + q) * 128:(hb * 4 + q + 1) * 128]) for q in range(4)],
                              ident, [("xsA", sl), "tabs"], [PK(BT[hb])])
                        yield
                        evac(xT[gs][:, hb * 4:(hb + 1) * 4, t * 128:(t + 1) * 128], q4v(bank(BT[hb])), PK(BT[hb]), ("xTA", gs, t, hb))
                        yield

            def group_A(sq, g):
                G = sq * NG + g
                gs = G % 2
                os_ = G % 2
                xTk = [("xTA", gs, t, hb) for t in range(4) for hb in range(2)]

                def fm_block(col, dst, dkey, scale=None):
                    b = next_acc()
                    O.acc(bank(b), [(win[:, kc, col:col + 128], xT[gs][:, kc, :]) for kc in range(8)], WINK + xTk, [PK(b)])
                    evac(dst, bank(b), PK(b), dkey, scale=scale)

                for j in range(4):
                    fm_block(j * 128, qT_st[os_][:, j, :], ("qTst", os_, j))
                    yield
                for j in range(4):
                    fm_block(512 + j * 128, kT_st[os_][:, j, :], ("kTst", os_, j))
                    yield
                for j in range(4):
                    fm_block(2048 + j * 128, aq_st[os_][:, j, :], ("aqst", os_, j), scale=0.125)
                    yield
                fm_block(2560, ak_st[os_][:], ("akst", os_))
                yield

                for t in range(4):
                    n = g * 4 + t
                    ks = (G * 4 + t) % 2
                    xk = [("xTA", gs, t, 0), ("xTA", gs, t, 1)]

                    def tm_mm(col, width):
                        b = next_acc()
                        O.acc(bank(b)[:, 0:width], [(xT[gs][:, kc, t * 128:(t + 1) * 128], win[:, kc, col:col + width]) for kc in range(8)],
                              WINK + xk, [PK(b)])
                        return b

                    b = tm_mm(512, 512)
                    O.tt(h8(kf[ks][:]), h8(bank(b)), dk[:, 0:8].unsqueeze(2).to_broadcast([128, 8, 64]), ALU.mult,
                         [("dk", 0)], [PK(b), ("kf", ks)])
                    O.tt(h8(kb[ks][:]), h8(bank(b)), dk[:, 8:16].unsqueeze(2).to_broadcast([128, 8, 64]), ALU.mult,
                         [("dk", 1)], [PK(b), ("kb", ks)])
                    b = tm_mm(1024, 512)
                    O.copy("act", v_st[os_][:, t, :], bank(b), [], [PK(b), ("vst", os_, t)])
                    b = tm_mm(1536, 512)
                    O.act(sg[ks][:], bank(b), AF.Silu, [], [PK(b), ("sgA", ks)])
                    O.tt(gg_st[os_][:, t, :], sg[ks][:], gnr[:], ALU.mult, [("sgA", ks), "gnr"], [("ggst", os_)])
                    b = tm_mm(2688, 128)
                    O.copy("act", av_st[os_][:, t, :].rearrange("p (g c) -> p g c", g=2)[:, :, 0:64],
                           bank(b)[:, 0:128].rearrange("p (g c) -> p g c", g=2), [], [PK(b), ("avst", os_)])
                    items = []
                    for j in range(4):
                        items.append((bank(KVF)[:, j * 128:(j + 1) * 128], kf[ks][:, j * 128:(j + 1) * 128],
                                      v_st[os_][:, t, j * 128:(j + 1) * 128], True, True))
                    for j in range(4):
                        items.append((bank(KVB)[:, j * 128:(j + 1) * 128], kb[ks][:, j * 128:(j + 1) * 128],
                                      v_st[os_][:, t, j * 128:(j + 1) * 128], True, True))
                    O.mms(items, [("kf", ks), ("kb", ks), ("vst", os_, t)], [PK(KVF), PK(KVB)])
                    scan_step(st_f, n % 2, rf_st[:, n, :], 0, bank(KVF), [PK(KVF)], "stf", "rfst")
                    O.copy("act", kvb[:, n, :], bank(KVB), [], [PK(KVB), "kvb"])
                    yield

                def st(dst, src, rkeys, tag):
                    O.dma("pool", dst, src, rkeys, [], tag)
                st(QT[G], fl(qT_st[os_][:]), [("qTst", os_, j) for j in range(4)], "stq%d" % os_)
                st(KT[G], fl(kT_st[os_][:]), [("kTst", os_, j) for j in range(4)], "stk%d" % os_)
                st(AQT[G], fl(aq_st[os_][:]), [("aqst", os_, j) for j in range(4)], "staq%d" % os_)
                st(VV[G], fl(v_st[os_][:]), [("vst", os_, t) for t in range(4)], "stv%d" % os_)
                st(GGS[G], fl(gg_st[os_][:]), [("ggst", os_)], "stg%d" % os_)
                st(AKT[sq][:, g * 512:(g + 1) * 512], ak_st[os_][:], [("akst", os_)], "stak%d" % os_)
                st(AV1[sq, g * 4:(g + 1) * 4].rearrange("t p c -> p t c"), av_st[os_][:],
                   [("avst", os_)], "stav%d" % os_)

            def run_ilA(gens):
                gens = list(gens)
                while gens:
                    for g_ in list(gens):
                        try:
                            next(g_)
                        except StopIteration:
                            gens.remove(g_)

            run_ilA([chain_X(0)])
            for sq in range(NSEQ):
                O.memsets([(st_f[0][:], 0.0)], [("stf", 0)])
                for g in range(NG):
                    gens = [group_A(sq, g)]
                    if sq * NG + g + 1 < NGT:
                        gens.append(chain_X(sq * NG + g + 1))
                    run_ilA(gens)
                O.memsets([(st_b[(NT - 1) % 2][:], 0.0)], [("stb", (NT - 1) % 2)])
                for n in range(NT - 1, -1, -1):
                    scan_step(st_b, n % 2, rb_st[:, n, :], 4, kvb[:, n, :], ["kvb"], "stb", "rbst")
                O.dma("pool", RF[sq], fl(rf_st[:]), ["rfst"], [], "strf")
                O.dma("pool", RB[sq], fl(rb_st[:]), ["rbst"], [], "strb")

        P.barrier()

        with contextlib.ExitStack() as pb:
            wo = sbuf(pb, "wo", [128, 8, D], BF16)
            attb = sbuf(pb, "attb", [128, 2, 3, 2, 512], BF16)
            identb = sbuf(pb, "identb", [128, 128], BF16)
            esk = sbuf(pb, "esk", [128, 8], F32)
            eskr = sbuf(pb, "eskr", [128, 8], F32)
            nhalf = sbuf(pb, "nhalf", [128, 8], F32)
            gbcol = sbuf(pb, "gbcol", [128, 16], F32)
            qTb = [sbuf(pb, "qTb%d" % i, [128, 4, 512], BF16) for i in range(2)]
            kTb = [sbuf(pb, "kTb%d" % i, [128, 4, 512], BF16) for i in range(2)]
            aqb = [sbuf(pb, "aqb%d" % i, [128, 4, 512], BF16) for i in range(2)]
            vb = [sbuf(pb, "vb%d" % i, [128, 4, 512], BF16) for i in range(2)]
            ggb = [sbuf(pb, "ggb%d" % i, [128, 4, 512], BF16) for i in range(2)]
            rfb = [sbuf(pb, "rfb%d" % i, [128, 4, 512], BF16) for i in range(2)]
            rbb = [sbuf(pb, "rbb%d" % i, [128, 4, 512], BF16) for i in range(2)]
            akp = [[[sbuf(pb, "akp%d%d%d" % (i, gk, pp), [128, 768], BF16) for pp in range(2)] for gk in range(2)] for i in range(2)]
            avb = [sbuf(pb, "avb%d" % i, [128, 6, 130], BF16) for i in range(2)]
            xsb = [sbuf(pb, "xsB%d" % i, [128, D], F32) for i in range(2)]
            Sbuf = sbuf(pb, "Sbuf", [128, 2, 4, 128], BF16)
            qf = sbuf(pb, "qf", [128, 4, 128], BF16)
            qb = sbuf(pb, "qb", [128, 4, 128], BF16)
            ysq = sbuf(pb, "ysq", [128, 512], F32)
            gst = sbuf(pb, "gst", [128, 64], F32)
            qfg = [sbuf(pb, "qfg%d" % i, [128, 4, 512], BF16) for i in range(2)]
            qbg = [sbuf(pb, "qbg%d" % i, [128, 4, 512], BF16) for i in range(2)]
            yr = sbuf(pb, "yr", [128, 512], F32)
            yrb = sbuf(pb, "yrb", [128, 512], BF16)
            yab = sbuf(pb, "yab", [128, 512], BF16)
            Pb = sbuf(pb, "Pb", [128, 3, 8, 128], BF16)
            den = sbuf(pb, "den", [128, 16], F32)
            ya = sbuf(pb, "ya", [128, 512], F32)
            catT = [sbuf(pb, "catT%d" % i, [128, 8, 128], BF16) for i in range(2)]
            r_t = sbuf(pb, "r_t", [128, D], F32)
            h_t = [sbuf(pb, "h_t%d" % i, [128, D], F32) for i in range(2)]
            hT_st = [sbuf(pb, "hTst%d" % i, [128, 8, 512], BF16) for i in range(2)]
            st6 = sbuf(pb, "st6B", [128, 12], F32)
            mv = sbuf(pb, "mvB", [128, 8], F32)

            w_out_v = w_out.rearrange("(kc p) f -> p kc f", p=128)
            WOK = [("wo", 0), ("wo", 4)]
            for kc in range(0, 8, 4):
                O.dma("pool", wo[:, kc:kc + 4, :], w_out_v[:, kc:kc + 4, :], [], [("wo", kc)], "wo")
            O.dma("pool", attb[:].rearrange("p a b c d -> p (a b c d)"), attb_d.rearrange("p a b c d -> p (a b c d)"), [], ["attb"], "cb2")
            O.dma("sp", eskr[:], sink[0].partition_broadcast(128), [], ["eskr"], "cb3")
            O.act(esk[:], eskr[:], AF.Exp, ["eskr"], ["esk"])
            O.copy("dve", identb[:], ident, ["tabs"], ["identb"])
            O.memsets([(nhalf[:], -0.5)], ["nhalf"])
            AKPK = lambda i: [("akp", i, gk, pp) for gk in range(2) for pp in range(2)]
            O.memsets([(akp[i][gk][pp][:], 0.0) for i in range(2) for gk in range(2) for pp in range(2)], AKPK(0) + AKPK(1))
            LNPK = [("lnpx", 0), ("lnpx", 1)]

            def load_group_B(G):
                sq, g = divmod(G, NG)
                bs = G % 2
                ld = lambda dst, src, key, tag: O.dma("sp", dst, src, [], [key], tag)
                ld(fl(qTb[bs][:]), QT[G], ("qTb", bs), "lq%d" % bs)
                ld(fl(kTb[bs][:]), KT[G], ("kTb", bs), "lk%d" % bs)
                ld(fl(rfb[bs][:]), RF[sq][:, g * 2048:(g + 1) * 2048], ("rfb", bs), "lrf%d" % bs)
                ld(fl(rbb[bs][:]), RB[sq][:, g * 2048:(g + 1) * 2048], ("rbb", bs), "lrb%d" % bs)
                ld(fl(vb[bs][:]), VV[G], ("vb", bs), "lv%d" % bs)
                ld(fl(aqb[bs][:]), AQT[G], ("aqb", bs), "laq%d" % bs)
                lo = max(0, g * 512 - 128)
                hi = min(S, g * 512 + 640)
                off = lo - (g * 512 - 128)
                for gk in range(2):
                    for pp in range(2):
                        ld(akp[bs][gk][pp][pp * 64:(pp + 1) * 64, off:off + (hi - lo)], AKT[sq][gk * 64:(gk + 1) * 64, lo:hi],
                           ("akp", bs, gk, pp), "lak%d%d%d" % (bs, gk, pp))
                tlo = max(0, g * 4 - 1)
                thi = min(NT, g * 4 + 5)
                toff = tlo - (g * 4 - 1)
                ld(avb[bs][:, toff:toff + (thi - tlo), :], AV1[sq, tlo:thi].rearrange("t p c -> p t c"), ("avb", bs), "lav%d" % bs)
                ld(fl(ggb[bs][:]), GGS[G], ("ggb", bs), "lg%d" % bs)
                q4 = lambda a: a.rearrange("p j (t c) -> p j t c", t=4)
                O.tt(q4(qfg[bs][:]), q4(qTb[bs][:]), Gf[:].unsqueeze(2).to_broadcast([128, 4, 4, 128]), ALU.mult,
                     [("qTb", bs), "Gf"], [("qfg", bs)])
                O.tt(q4(qbg[bs][:]), q4(qTb[bs][:]), Gb[:].unsqueeze(2).to_broadcast([128, 4, 4, 128]), ALU.mult,
                     [("qTb", bs), "Gb"], [("qbg", bs)])

            def chain_R(G, t):
                sq, g = divmod(G, NG)
                bs = G % 2
                gt = G * 4 + t
                cs_ = gt % 2
                xsl = gt % 2
                c0, c1 = t * 128, (t + 1) * 128
                O.dma("sp", xsb[xsl][:], x[gt * 128:(gt + 1) * 128, :], [], [("xsB", xsl)], "xsB%d" % xsl)
                items = []
                for j in range(4):
                    for pp in range(2):
                        items.append((bank(2 + pp)[:, j * 128:(j + 1) * 128], kTb[bs][pp * 64:(pp + 1) * 64, j, c0:c1],
                                      qTb[bs][pp * 64:(pp + 1) * 64, j, c0:c1], True, True))
                O.mms(items, [("kTb", bs), ("qTb", bs)], [PK(2), PK(3)])
                yield
                for pp in range(2):
                    O.tt(fl(Sbuf[:, pp, :, :]), bank(2 + pp), fl(maskT[:, pp, :, :]), ALU.mult, ["maskT"], [PK(2 + pp), ("Sbuf", pp)])
                    yield
                items = []
                for j in range(4):
                    yo = bank(2)[:, j * 128:(j + 1) * 128]
                    items.append((yo, qfg[bs][:, j, c0:c1], rfb[bs][:, t, j * 128:(j + 1) * 128], True, False))
                    items.append((yo, qbg[bs][:, j, c0:c1], rbb[bs][:, t, j * 128:(j + 1) * 128], False, True))
                    items.append((bank(2)[:, j * 128:j * 128 + 64], Sbuf[:, 0, j, :], vb[bs][:, t, j * 128:j * 128 + 64], False, True, True))
                    items.append((bank(2)[:, j * 128 + 64:(j + 1) * 128], Sbuf[:, 1, j, :], vb[bs][:, t, j * 128 + 64:(j + 1) * 128],
                                  False, True, True))
                O.mms(items, [("qfg", bs), ("qbg", bs), ("rfb", bs), ("rbb", bs), ("Sbuf", 0), ("Sbuf", 1), ("vb", bs)], [PK(2)])
                yield
                y3 = h8(bank(2))
                O.reduce(gst[:, 0:8], y3, ALU.add, [], [PK(2), "gs1"])
                yield
                O.act(ysq[:], bank(2), AF.Square, [], [PK(2), "ysq"])
                yield
                O.reduce(gst[:, 8:16], h8(ysq[:]), ALU.add, ["ysq"], ["gs2"])
                yield
                O.tt(gst[:, 16:24], gst[:, 0:8], gst[:, 0:8], ALU.mult, ["gs1"], ["gt"])
                yield
                O.stt(gst[:, 24:32], gst[:, 8:16], 64.0, gst[:, 16:24], ALU.mult, ALU.subtract, ["gs2", "gt"], ["gu"])
                yield
                O.ts(gst[:, 32:40], gst[:, 24:32], 1.0 / 4096.0, GN_EPS, ALU.mult, ALU.add, ["gu"], ["gve"])
                yield
                O.act(gst[:, 56:64], gst[:, 32:40], AF.Ln, ["gve"], ["glnve"])
                yield
                O.act(gst[:, 40:48], gst[:, 56:64], AF.Exp, ["glnve"], ["grstd"], scale=-0.5)
                yield
                O.stt(gst[:, 48:56], gst[:, 0:8], -1.0 / 64.0, gst[:, 40:48], ALU.mult, ALU.mult, ["gs1", "grstd"], ["gnmr"])
                yield
                O.tt(h8(yr[:]), y3, gst[:, 40:48].unsqueeze(2).to_broadcast([128, 8, 64]), ALU.mult, ["grstd"], [PK(2), "yr"])
                yield
                O.tt(h8(yr[:]), h8(yr[:]), gst[:, 48:56].unsqueeze(2).to_broadcast([128, 8, 64]), ALU.add, ["yr", "gnmr"], ["yr"])
                yield
                O.tt(yrb[:], yr[:], ggb[bs][:, t, :], ALU.mult, ["yr", ("ggb", bs)], ["yrb"])
                yield
                b3 = bank(3).bitcast(BF16)
                O.trs([(b3[:, q * 128:(q + 1) * 128], yrb[:, q * 128:(q + 1) * 128]) for q in range(4)], identb[:], ["yrb", "identb"], [PK(3)])
                yield
                O.copy("act", catT[cs_][:, 0:4, :], q4v(b3[:, 0:512]), [], [PK(3), ("catT", cs_, 0)])
                yield

            def chain_A(G, t):
                sq, g = divmod(G, NG)
                bs = G % 2
                n = g * 4 + t
                gt = G * 4 + t
                cs_ = gt % 2
                c0, c1 = t * 128, (t + 1) * 128
                kbs = [kb_ for kb_ in range(3) if 0 <= n + kb_ - 1 < NT]
                si = 0
                for gk in range(2):
                    for kb_ in kbs:
                        sb_ = si % 2
                        si += 1
                        kc0 = (t + kb_) * 128
                        items = [(bank(sb_), identb[:], attb[:, 0, kb_, gk, :], True, True)]
                        for pp in range(2):
                            items.append((bank(sb_)[:, pp * 256:(pp + 1) * 256], akp[bs][gk][pp][:, kc0:kc0 + 128],
                                          aqb[bs][:, 2 * gk:2 * gk + 2, c0:c1], False, True, True))
                        O.mms(items, AKPK(bs) + [("aqb", bs), "attb", "identb"], [PK(sb_)])
                        yield
                        yield
                        pview = Pb[:, kb_, 4 * gk:4 * gk + 4, :].rearrange("p (bi pp) q -> p pp bi q", pp=2)
                        sview = bank(sb_).rearrange("p (pp bi q) -> p pp bi q", pp=2, bi=2)
                        O.act(pview, sview, AF.Exp, [], [PK(sb_), ("Pb", gk)])
                        yield
                    items = []
                    for hh in range(4):
                        h = 4 * gk + hh
                        for i_, kb_ in enumerate(kbs):
                            items.append((bank(4 + gk)[:, hh * 65:(hh + 1) * 65], Pb[:, kb_, h, :], avb[bs][:, t + kb_, gk * 65:(gk + 1) * 65],
                                          i_ == 0, i_ == len(kbs) - 1))
                    O.mms(items, [("Pb", gk), ("avb", bs)], [PK(4 + gk)])
                    yield
                    pv3 = bank(4 + gk)[:, 0:260].rearrange("p (h c) -> p h c", h=4)
                    O.tt(den[:, gk * 4:(gk + 1) * 4].unsqueeze(2), pv3[:, :, 64:65], esk[:, gk * 4:(gk + 1) * 4].unsqueeze(2), ALU.add,
                         ["esk"], [PK(4 + gk), ("den", gk)])
                    yield
                    O.recip(den[:, 8 + gk * 4:8 + (gk + 1) * 4], den[:, gk * 4:(gk + 1) * 4], [("den", gk)], [("rden", gk)])
                    yield
                    O.tt(yab[:, gk * 256:(gk + 1) * 256].rearrange("p (h c) -> p h c", h=4), pv3[:, :, 0:64],
                         den[:, 8 + gk * 4:8 + (gk + 1) * 4].unsqueeze(2).to_broadcast([128, 4, 64]), ALU.mult,
                         [("rden", gk)], [PK(4 + gk), ("ya", gk)])
                    yield
                b0 = bank(0).bitcast(BF16)
                O.trs([(b0[:, q * 128:(q + 1) * 128], yab[:, q * 128:(q + 1) * 128]) for q in range(4)], identb[:],
                      [("ya", 0), ("ya", 1), "identb"], [PK(0)])
                yield
                O.copy("act", catT[cs_][:, 4:8, :], q4v(b0[:, 0:512]), [], [PK(0), ("catT", cs_, 1)])
                yield

            def chain_O(G, t):
                gt = G * 4 + t
                cs_ = gt % 2
                xsl = gt % 2
                hsl = gt % 2
                hs = G % 2
                c0, c1 = t * 128, (t + 1) * 128
                for nb in range(2):
                    O.acc(bank(6 + nb), [(catT[cs_][:, kc, :], wo[:, kc, nb * 512:(nb + 1) * 512]) for kc in range(8)],
                          [("catT", cs_, 0), ("catT", cs_, 1)] + WOK, [PK(6 + nb)])
                    yield
                    yield
                    yield
                    O.stt(r_t[:, nb * 512:(nb + 1) * 512], xsb[xsl][:, nb * 512:(nb + 1) * 512], ALPHA, bank(6 + nb), ALU.mult, ALU.add,
                          [("xsB", xsl)], [PK(6 + nb), ("r_t", nb)])
                    yield
                rk = [("r_t", 0), ("r_t", 1)]
                O.bnstats([(st6[:, 0:6], r_t[:, 0:512]), (st6[:, 6:12], r_t[:, 512:1024])], rk, ["Bst"])
                yield
                O.bnaggr(mv[:, 0:2], st6[:, 0:12], ["Bst"], ["Bmv"])
                yield
                O.ts(mv[:, 2:3], mv[:, 1:2], LN_EPS, None, ALU.add, None, ["Bmv"], ["Bve"])
                yield
                O.act(mv[:, 5:6], mv[:, 2:3], AF.Ln, ["Bve"], ["Blnve"])
                yield
                O.act(mv[:, 3:4], mv[:, 5:6], AF.Exp, ["Blnve"], ["Brs"], scale=-0.5)
                yield
                O.stt(mv[:, 4:5], mv[:, 0:1], -1.0, mv[:, 3:4], ALU.mult, ALU.mult, ["Bmv", "Brs"], ["Bnm"])
                yield
                hk = ("h_t", hsl)
                O.act(h_t[hsl][:], r_t[:], AF.Identity, rk + ["Brs", "Bnm"], [hk], scale=mv[:, 3:4], bias=mv[:, 4:5])
                yield
                for hb in range(2):
                    O.trs([(bank(6 + hb)[:, q * 128:(q + 1) * 128], h_t[hsl][:, (hb * 4 + q) * 128:(hb * 4 + q + 1) * 128]) for q in range(4)],
                          ident, [hk, "tabs"], [PK(6 + hb)])
                    yield
                for hb in range(2):
                    O.acts_sb([(hT_st[hs][:, hb * 4 + q, c0:c1], bank(6 + hb)[:, q * 128:(q + 1) * 128], AF.Identity,
                                gbcol[:, hb * 4 + q:hb * 4 + q + 1], gbcol[:, 8 + hb * 4 + q:8 + hb * 4 + q + 1]) for q in range(4)],
                              [("gbcol", 0), ("gbcol", 1)], [PK(6 + hb), ("hTst", hs)])
                    yield
                O.dma("pool", H32[gt], h_t[hsl][:], [hk], [], "sth%d" % hsl)
                if t == 3:
                    O.dma("pool", HT[G], fl(hT_st[hs][:]), [("hTst", hs)], [], "stht%d" % hs)

            def run_interleaved(gens):
                gens = list(gens)
                while gens:
                    for g_ in list(gens):
                        try:
                            next(g_)
                        except StopIteration:
                            gens.remove(g_)

            load_group_B(0)
            O.dma("sp", gbcol[:, 0:8], ln1g[0].rearrange("(kc p) -> p kc", p=128), [], [("gbcol", 0)], "cb4", slow=True)
            O.dma("sp", gbcol[:, 8:16], ln1b[0].rearrange("(kc p) -> p kc", p=128), [], [("gbcol", 1)], "cb5", slow=True)
            prev = None
            for G in range(NGT):
                if G + 1 < NGT:
                    load_group_B(G + 1)
                for t in range(4):
                    gens = [chain_R(G, t), chain_A(G, t)]
                    if prev is not None:
                        gens.append(chain_O(*prev))
                    run_interleaved(gens)
                    prev = (G, t)
            run_interleaved([chain_O(*prev)])

        P.barrier()

        pcd = top.enter_context(contextlib.ExitStack())
        wd = sbuf(pcd, "wd", [128, NJB, D], BF16)
        wpg = sbuf(pcd, "wpg", [128, 8, D], BF16)
        with contextlib.ExitStack() as pc:
            wg = sbuf(pc, "wg", [128, 8, FF], BF16)
            wu = sbuf(pc, "wu", [128, 8, FF], BF16)
            hTc = [sbuf(pc, "hTc%d" % i, [128, 8, 512], BF16) for i in range(2)]
            act_st = [sbuf(pc, "actst%d" % i, [128, NJB // 2, 512], BF16) for i in range(2)]
            sgc = [sbuf(pc, "sgc%d" % i, [128, 512], F32) for i in range(3)]
            w_g_v = w_g.rearrange("(kc p) f -> p kc f", p=128)
            w_u_v = w_u.rearrange("(kc p) f -> p kc f", p=128)
            NCH = (NJB + 3) // 4
            for c in range(NCH):
                cs = slice(c * 512, min(FF, (c + 1) * 512))
                O.dma("pool", wg[:, :, cs], w_g_v[:, :, cs], [], [("wg", c)], "wg%d" % c)
                O.dma("pool", wu[:, :, cs], w_u_v[:, :, cs], [], [("wu", c)], "wu%d" % c)
            w_d_v = w_d.rearrange("(jb p) f -> p jb f", p=128)
            for j0 in range(0, NJB, 2):
                O.dma("pool", wd[:, j0:j0 + 2, :], w_d_v[:, j0:j0 + 2, :], [], [("wd", j0)], "wd")
            w_pg_v = w_pg.rearrange("(kc p) f -> p kc f", p=128)
            for kc in range(0, 8, 4):
                O.dma("pool", wpg[:, kc:kc + 4, :], w_pg_v[:, kc:kc + 4, :], [], [("wpg", kc)], "wpg")

            def load_C1(G):
                sl = G % 2
                O.dma("sp", fl(hTc[sl][:]), HT[G], [], [("hTc", sl)], "lht%d" % sl)

            cc = {"pi": 0}

            def group_C1(G):
                sl = G % 2
                hlf = NJB // 2
                for jb in range(NJB):
                    pi = cc["pi"]
                    cc["pi"] += 1
                    ba, bb = (pi % 4) * 2, (pi % 4) * 2 + 1
                    ssl = pi % 3
                    O.acc(bank(ba), [(wg[:, kc, jb * 128:(jb + 1) * 128], hTc[sl][:, kc, :]) for kc in range(8)], [("wg", jb // 4), ("hTc", sl)], [PK(ba)])
                    O.acc(bank(bb), [(wu[:, kc, jb * 128:(jb + 1) * 128], hTc[sl][:, kc, :]) for kc in range(8)], [("wu", jb // 4), ("hTc", sl)], [PK(bb)])
                    O.act(sgc[ssl][:], bank(ba), AF.Silu, [], [PK(ba), ("sgc", ssl)])
                    hh_ = jb // hlf
                    O.tt(act_st[hh_][:, jb - hh_ * hlf, :], sgc[ssl][:], bank(bb), ALU.mult, [("sgc", ssl)], [PK(bb), ("actst", hh_)])
                    if jb == hlf - 1:
                        O.dma("pool", ACTS[G][:, 0:hlf * 512], fl(act_st[0][:]), [("actst", 0)], [], "sta0")
                O.dma("pool", ACTS[G][:, hlf * 512:NJB * 512], fl(act_st[1][:]), [("actst", 1)], [], "sta1")

            load_C1(0)
            for G in range(NGT):
                if G + 1 < NGT:
                    load_C1(G + 1)
                group_C1(G)

        P.barrier()

        with contextlib.ExitStack() as pd_:
            wpe = sbuf(pd_, "wpe", [128, 2, D], BF16)
            lnp2 = sbuf(pd_, "lnp2", [128, 2, D], F32)
            actc = [sbuf(pd_, "actc%d" % i, [128, NJB, 512], BF16) for i in range(2)]
            hTd = [sbuf(pd_, "hTd%d" % i, [128, 8, 512], BF16) for i in range(2)]
            h32 = [sbuf(pd_, "h32%d" % i, [128, D], F32) for i in range(2)]
            pt = [sbuf(pd_, "pt%d" % i, [128, PLE], F32) for i in range(2)]
            pT = [sbuf(pd_, "pT%d" % i, [128, 2, 128], BF16) for i in range(2)]
            sig = sbuf(pd_, "sig", [128, D], F32)
            z_t = [sbuf(pd_, "z_t%d" % i, [128, D], F32) for i in range(2)]
            nhalf2 = sbuf(pd_, "nhalf2", [128, 1], F32)
            agb = sbuf(pd_, "agb", [128, 2, D], F32)
            o_t = [sbuf(pd_, "o_t%d" % i, [128, D], F32) for i in range(2)]
            st6 = sbuf(pd_, "st6D", [128, 12], F32)
            mv = sbuf(pd_, "mvD", [128, 8], F32)
            sdv = sbuf(pd_, "sdD", [128, 1], F32)
            rsv = sbuf(pd_, "rsD", [128, 1], F32)

            WDK = []
            WPGK = []
            O.dma("pool", wpe[:], w_pe.rearrange("(c p) f -> p c f", p=128), [], ["wpe"], "wpe")
            O.dma("sp", lnp2[:, 0, :], ln2g[0].partition_broadcast(128), [], [("lnpx", 0)], "cd0")
            O.dma("sp", lnp2[:, 1, :], ln2b[0].partition_broadcast(128), [], [("lnpx", 1)], "cd1")
            LNPK2 = [("lnpx", 0), ("lnpx", 1)]
            O.memsets([(nhalf2[:], -0.5)], ["nhalf2"])
            O.dma("sp", agb[:, 0, :], ln1g[0].partition_broadcast(128), [], [("agb", 0)], "cd2")
            O.dma("sp", agb[:, 1, :], ln1b[0].partition_broadcast(128), [], [("agb", 1)], "cd3")
            O.ts(agb[:, 0, :], agb[:, 0, :], ALPHA, None, ALU.mult, None, [("agb", 0)], [("agb", 0)])
            O.ts(agb[:, 1, :], agb[:, 1, :], ALPHA, None, ALU.mult, None, [("agb", 1)], [("agb", 1)])
            hlf = NJB // 2

            def load_C2(G):
                sl = G % 2
                O.dma("sp", fl(actc[sl][:, 0:hlf, :]), ACTS[G][:, 0:hlf * 512], [], [("actc", sl, 0)], "lac%d0" % sl)
                O.dma("sp", fl(actc[sl][:, hlf:NJB, :]), ACTS[G][:, hlf * 512:NJB * 512], [], [("actc", sl, 1)], "lac%d1" % sl)
                O.dma("sp", fl(hTd[sl][:]), HT[G], [], [("hTd", sl)], "lhd%d" % sl)

            def load_tile_C2(gt):
                sl = gt % 2
                O.dma("sp", h32[sl][:], H32[gt], [], [("h32", sl)], "lh32%d" % sl)
                O.dma("sp", pt[sl][:], pin[gt * 128:(gt + 1) * 128, :], [], [("pt", sl)], "lpt%d" % sl)

            def chain_M(G, t):
                sl = G % 2
                gt = G * 4 + t
                tsl = gt % 2
                zs = gt % 2
                c0, c1 = t * 128, (t + 1) * 128
                O.trs([(bank(6)[:, c * 128:(c + 1) * 128], pt[tsl][:, c * 128:(c + 1) * 128]) for c in range(2)], ident,
                      [("pt", tsl), "tabs"], [PK(6)])
                yield
                O.copy("act", pT[tsl][:], bank(6)[:, 0:256].rearrange("p (a b) -> p a b", a=2), [], [PK(6), ("pT", tsl)])
                yield
                for nb in range(2):
                    cs = slice(nb * 512, (nb + 1) * 512)
                    O.acc(bank(2 + nb), [(hTd[sl][:, kc, c0:c1], wpg[:, kc, cs]) for kc in range(8)], [("hTd", sl)] + WPGK, [PK(2 + nb)])
                    yield
                    O.acc(bank(4 + nb), [(pT[tsl][:, c, :], wpe[:, c, cs]) for c in range(2)], [("pT", tsl), "wpe"], [PK(4 + nb)])
                    yield
                    O.acc(bank(nb), [(actc[sl][:, jb, c0:c1], wd[:, jb, cs]) for jb in range(NJB)],
                          [("actc", sl, 0), ("actc", sl, 1)] + WDK, [PK(nb)])
                    yield
                for nb in range(2):
                    cs = slice(nb * 512, (nb + 1) * 512)
                    O.act(sig[:, cs], bank(2 + nb), AF.Sigmoid, [], [PK(2 + nb), ("sig", nb)])
                    yield
                    O.tt(sig[:, cs], sig[:, cs], bank(4 + nb), ALU.mult, [("sig", nb)], [PK(4 + nb), ("sig", nb)])
                    yield
                    O.tt(z_t[zs][:, cs], h32[tsl][:, cs], agb[:, 0, cs], ALU.mult, [("h32", tsl), ("agb", 0)], [("z_t", zs, nb)])
                    yield
                    O.tt(sig[:, cs], sig[:, cs], agb[:, 1, cs], ALU.add, [("sig", nb), ("agb", 1)], [("sig", nb)])
                    yield
                    O.tt(z_t[zs][:, cs], z_t[zs][:, cs], bank(nb), ALU.add, [("z_t", zs, nb)], [PK(nb), ("z_t", zs, nb)])
                    yield
                    O.tt(z_t[zs][:, cs], z_t[zs][:, cs], sig[:, cs], ALU.add, [("z_t", zs, nb), ("sig", nb)], [("z_t", zs, nb)])
                    yield

            def chain_L(G, t):
                gt = G * 4 + t
                tsl = gt % 2
                zs = gt % 2
                zk = [("z_t", zs, 0), ("z_t", zs, 1)]
                z = z_t[zs]
                O.bnstats([(st6[:, 0:6], z[:, 0:512]), (st6[:, 6:12], z[:, 512:1024])], zk, ["Dst"])
                yield
                O.bnaggr(mv[:, 0:2], st6[:, 0:12], ["Dst"], ["Dmv"])
                yield
                O.ts(mv[:, 2:3], mv[:, 1:2], LN_EPS, None, ALU.add, None, ["Dmv"], ["Dve"])
                yield
                O.tt(mv[:, 3:4], mv[:, 2:3], nhalf2[:, 0:1], ALU.pow, ["Dve", "nhalf2"], ["Drs"], eng="pool")
                yield
                O.stt(mv[:, 4:5], mv[:, 0:1], -1.0, mv[:, 3:4], ALU.mult, ALU.mult, ["Dmv", "Drs"], ["Dnm"])
                yield
                ok = ("o_t", tsl)
                O.act(o_t[tsl][:], z[:], AF.Identity, zk + ["Drs", "Dnm"], [ok], scale=mv[:, 3:4], bias=mv[:, 4:5])
                yield
                O.tt(o_t[tsl][:], o_t[tsl][:], lnp2[:, 0, :], ALU.mult, [ok] + LNPK2, [ok])
                yield
                O.tt(o_t[tsl][:], o_t[tsl][:], lnp2[:, 1, :], ALU.add, [ok] + LNPK2, [ok])
                yield
                fo = O.dma("pool", out[gt * 128:(gt + 1) * 128, :], o_t[tsl][:], [ok], [], "sto%d" % tsl)
                final_ops.append(fo)

            def run_il(gens):
                gens = list(gens)
                while gens:
                    for g_ in list(gens):
                        try:
                            next(g_)
                        except StopIteration:
                            gens.remove(g_)

            load_C2(0)
            load_tile_C2(0)
            prev = None
            for G in range(NGT):
                if G + 1 < NGT:
                    load_C2(G + 1)
                for t in range(4):
                    gt = G * 4 + t
                    if gt + 1 < NTT:
                        load_tile_C2(gt + 1)
                    gens = [chain_M(G, t)]
                    if prev is not None:
                        gens.append(chain_L(*prev))
                    run_il(gens)
                    prev = (G, t)
            run_il([chain_L(*prev)])

        P.emit(final_ops=final_ops[-2:])
    return nc


_NC_CACHE = {}


def kernel(x, p, w_in, ret_decay_fwd, ret_decay_bwd, ret_gn_gain, attn_sink, w_out,
           ln1_gain, ln1_bias, w_ffn_gate, w_ffn_up, w_ffn_down, w_ple_proj, w_ple_gate,
           ln2_gain, ln2_bias, _debug=False):
    x = np.asarray(x)
    B, S, _ = x.shape
    ncores = min(8, B)
    NSEQ = B // ncores
    key = (NSEQ, S, _debug)
    if key not in _NC_CACHE:
        _NC_CACHE[key] = build(NSEQ, S, debug=_debug)
    nc = _NC_CACHE[key]
    tabs, colv, attb = _const_tables()
    f = lambda a: np.ascontiguousarray(np.asarray(a, dtype=np.float32))
    shared = {
        "w_in": f(w_in)[0], "ret_decay_fwd": f(ret_decay_fwd), "ret_decay_bwd": f(ret_decay_bwd),
        "ret_gn_gain": f(ret_gn_gain), "attn_sink": f(attn_sink), "w_out": f(w_out)[0],
        "ln1_gain": f(ln1_gain), "ln1_bias": f(ln1_bias), "w_ffn_gate": f(w_ffn_gate)[0],
        "w_ffn_up": f(w_ffn_up)[0], "w_ffn_down": f(w_ffn_down)[0], "w_ple_proj": f(w_ple_proj)[0],
        "w_ple_gate": f(w_ple_gate)[0], "ln2_gain": f(ln2_gain), "ln2_bias": f(ln2_bias),
        "c_tabs": tabs, "c_colv": colv, "c_attb": attb,
    }
    xf = f(x).reshape(ncores, NSEQ * S, D)
    pf = f(p)[0].reshape(ncores, NSEQ * S, PLE)
    in_maps = []
    for c in range(ncores):
        m = dict(shared)
        m["x"] = xf[c]
        m["p"] = pf[c]
        in_maps.append(m)
    res = run_bass_kernel_spmd(nc, in_maps, core_ids=list(range(ncores)))
    if _debug:
        return res
    outp = np.stack([np.asarray(r["out"]) for r in res.results], axis=0)
    return outp.reshape(B, S, D).astype(np.float32)
```
